# Optimizing a Trainium2 kernel written in Bass

```python
import jax, jax.numpy as jnp
from jax import lax
import numpy as np

D_MODEL = 1024
BATCH = 1
SEQ = 16384
DEPTH = 1
DEC_BATCH = 128
DEC_SEQ = 4
PAST_LEN = 8192
PAGE_SIZE = 128

N_HEADS_A = D_MODEL // 128
HEAD_DIM_A = 64
ATTN_WIDTH = N_HEADS_A * HEAD_DIM_A
DILATED_GROUPS = ((128, 1), (512, 4), (2048, 16))
WINDOW = 2048
N_HEADS_B = D_MODEL // 256
DK_B = 128
DV_B = 128
HK_B = N_HEADS_B * DK_B
HGRN_WIDTH = N_HEADS_B * DV_B
CHUNK_B = 64
MIX_WIDTH = ATTN_WIDTH + HGRN_WIDTH
SPLIT_POINTS = (ATTN_WIDTH, 2 * ATTN_WIDTH, 3 * ATTN_WIDTH, 3 * ATTN_WIDTH + HK_B,
                3 * ATTN_WIDTH + 2 * HK_B, 3 * ATTN_WIDTH + 2 * HK_B + HGRN_WIDTH)
IN_COLS = 3 * ATTN_WIDTH + 2 * HK_B + 2 * HGRN_WIDTH
D_FF = 256 * ((8 * D_MODEL // 3 + 255) // 256)
EPS = 1e-6

kernel_name = "hybrid_dilated_attn_hgrn2_macaron_step"


def rms_norm(x, g):
    xf = x.astype(jnp.float32)
    y = xf * lax.rsqrt(jnp.mean(xf * xf, axis=-1, keepdims=True) + EPS)
    return (y * g.astype(jnp.float32)).astype(x.dtype)


def alibi_slopes(n):
    return jnp.asarray([2.0 ** (-8.0 * (h + 1) / n) for h in range(n)], jnp.float32)


def swiglu(h, w_gate_up, w_down):
    g, u = jnp.split(h @ w_gate_up, 2, axis=-1)
    return (jax.nn.silu(g) * u) @ w_down


def dilated_attn_prompt(q, k, v, window, dil, slopes):
    B, S, H, E = q.shape
    L = window // dil
    Sp = -(-S // window) * window
    n = Sp // dil
    nb = n // L

    def to_blocks(t):
        t = jnp.pad(t, ((0, 0), (0, Sp - S), (0, 0), (0, 0))).reshape(B, n, dil, H, E)
        return t.transpose(0, 2, 1, 3, 4).reshape(B, dil, nb, L, H, E)

    def with_prev(t):
        prev = jnp.pad(t, ((0, 0), (0, 0), (1, 0), (0, 0), (0, 0), (0, 0)))[:, :, :-1]
        return jnp.concatenate([prev, t], axis=3)

    qb = to_blocks(q)
    kk = with_prev(to_blocks(k))
    vv = with_prev(to_blocks(v))
    s = jnp.einsum('brnqhe,brnkhe->brnhqk', qb, kk, preferred_element_type=jnp.float32)
    qi = jnp.arange(L)[:, None]
    ki = jnp.arange(2 * L)[None, :]
    steps = qi + L - ki
    band = (steps >= 0) & (steps <= L)
    first = (jnp.arange(nb)[:, None, None] == 0) & (ki < L)[None]
    valid = band[None] & ~first
    dist = (steps * dil).astype(jnp.float32)
    s = s - slopes[:, None, None] * dist
    s = jnp.where(valid[:, None], s, -jnp.inf)
    lse = jax.nn.logsumexp(s, axis=-1)
    p = jnp.exp(s - lse[..., None])
    o = jnp.einsum('brnhqk,brnkhe->brnqhe', p, vv.astype(jnp.float32))
    o = o.reshape(B, dil, n, H, E).transpose(0, 2, 1, 3, 4).reshape(B, Sp, H, E)[:, :S]
    lse = lse.transpose(0, 1, 2, 4, 3).reshape(B, dil, n, H).transpose(0, 2, 1, 3).reshape(B, Sp, H)[:, :S]
    return o, lse


def dilated_attn_sample(q, kcat, vcat, window, dil, slopes):
    T = q.shape[1]
    P = kcat.shape[1] - T
    j = jnp.arange(window // dil + 1)
    idx = P + jnp.arange(T)[:, None] - j[None, :] * dil
    valid = idx >= 0
    idxc = jnp.maximum(idx, 0)
    kg = kcat[:, idxc]
    vg = vcat[:, idxc]
    s = jnp.einsum('bthe,btjhe->bhtj', q, kg, preferred_element_type=jnp.float32)
    s = s - slopes[:, None, None] * (j * dil).astype(jnp.float32)[None, None, :]
    s = jnp.where(valid[None, None], s, -jnp.inf)
    lse = jax.nn.logsumexp(s, axis=-1)
    p = jnp.exp(s - lse[..., None])
    o = jnp.einsum('bhtj,btjhe->bthe', p, vg.astype(jnp.float32))
    return o, lse.transpose(0, 2, 1)


def merge_by_denominator(outs, lses):
    w = jax.nn.softmax(jnp.stack(lses, 0), axis=0)
    return jnp.einsum('gbsh,gbshe->bshe', w, jnp.stack(outs, 0))


def hgrn_chunk(state, q, k, v, logf):
    C = q.shape[1]
    b = jnp.cumsum(logf, axis=1)
    causal = jnp.tril(jnp.ones((C, C), bool))[None, :, :, None, None]
    rel = b[:, :, None] - b[:, None, :]
    decay = jnp.exp(jnp.where(causal, rel, -jnp.inf))
    a = jnp.einsum('bthk,btshk,bshk->bhts', q, decay, k)
    o = jnp.einsum('bhts,bshv->bthv', a, v)
    o = o + jnp.einsum('bthk,bhkv->bthv', q * jnp.exp(b), state)
    b_last = b[:, -1]
    new_state = jnp.exp(b_last)[..., None] * state + jnp.einsum(
        'bshk,bshv->bhkv', k * jnp.exp(b_last[:, None] - b), v)
    return o, new_state


def hgrn_prompt(q, k, v, logf):
    B, S, H, _ = q.shape
    nc = S // CHUNK_B

    def split(t):
        return t.reshape(B, nc, CHUNK_B, H, t.shape[-1]).swapaxes(0, 1)

    def step(st, xs):
        o, st = hgrn_chunk(st, *xs)
        return st, o

    init = jnp.zeros((B, H, DK_B, DV_B), jnp.float32)
    st, o = lax.scan(step, init, (split(q), split(k), split(v), split(logf)))
    return o.swapaxes(0, 1).reshape(B, S, H, DV_B), st


def mixer_inputs(h, w_in, q_norm, k_norm, lb):
    B, S, _ = h.shape
    qa, ka, va, qb, fb, ib, gb = jnp.split(h @ w_in, SPLIT_POINTS, axis=-1)
    qa = rms_norm(qa.reshape(B, S, N_HEADS_A, HEAD_DIM_A), q_norm) * (HEAD_DIM_A ** -0.5)
    ka = rms_norm(ka.reshape(B, S, N_HEADS_A, HEAD_DIM_A), k_norm)
    va = va.reshape(B, S, N_HEADS_A, HEAD_DIM_A)
    f = lb + (1.0 - lb) * jax.nn.sigmoid(fb.astype(jnp.float32))
    logf = jnp.log(f).reshape(B, S, N_HEADS_B, DK_B)
    kb = (1.0 - f).reshape(B, S, N_HEADS_B, DK_B)
    qb = jax.nn.silu(qb.astype(jnp.float32)).reshape(B, S, N_HEADS_B, DK_B)
    vb = ib.astype(jnp.float32).reshape(B, S, N_HEADS_B, DV_B)
    gb = gb.reshape(B, S, N_HEADS_B, DV_B)
    return (qa, ka, va), (qb, kb, vb, logf), gb


def mixer_output(attn_o, hgrn_o, g, attn_out_norm, hgrn_out_norm, w_out, dtype):
    B, S = attn_o.shape[:2]
    a = rms_norm(attn_o.reshape(B, S, ATTN_WIDTH), attn_out_norm)
    r = rms_norm(hgrn_o, hgrn_out_norm.reshape(N_HEADS_B, DV_B)) * jax.nn.silu(g.astype(jnp.float32))
    return jnp.concatenate([a, r.reshape(B, S, HGRN_WIDTH)], axis=-1).astype(dtype) @ w_out


def mixer_prompt(h, w_in, q_norm, k_norm, lb, attn_out_norm, hgrn_out_norm, w_out):
    S = h.shape[1]
    (qa, ka, va), (qb, kb, vb, logf), g = mixer_inputs(h, w_in, q_norm, k_norm, lb)
    slopes = alibi_slopes(N_HEADS_A)
    outs, lses = [], []
    for window, dil in DILATED_GROUPS:
        o, lse = dilated_attn_prompt(qa, ka, va, window, dil, slopes)
        outs.append(o)
        lses.append(lse)
    attn_o = merge_by_denominator(outs, lses)
    hgrn_o, st = hgrn_prompt(qb, kb, vb, logf)
    y = mixer_output(attn_o, hgrn_o, g, attn_out_norm, hgrn_out_norm, w_out, h.dtype)
    keep = min(WINDOW, S)
    return y, ka[:, S - keep:], va[:, S - keep:], st


def mixer_sample(h, ck, cv, st, w_in, q_norm, k_norm, lb, attn_out_norm, hgrn_out_norm, w_out):
    (qa, ka, va), (qb, kb, vb, logf), g = mixer_inputs(h, w_in, q_norm, k_norm, lb)
    slopes = alibi_slopes(N_HEADS_A)
    kcat = jnp.concatenate([ck.astype(ka.dtype), ka], axis=1)
    vcat = jnp.concatenate([cv.astype(va.dtype), va], axis=1)
    outs, lses = [], []
    for window, dil in DILATED_GROUPS:
        o, lse = dilated_attn_sample(qa, kcat, vcat, window, dil, slopes)
        outs.append(o)
        lses.append(lse)
    attn_o = merge_by_denominator(outs, lses)
    hgrn_o, new_st = hgrn_chunk(st.astype(jnp.float32), qb, kb, vb, logf)
    y = mixer_output(attn_o, hgrn_o, g, attn_out_norm, hgrn_out_norm, w_out, h.dtype)
    return y, ka, va, new_st


def setup_inputs(seed: int = 0) -> dict:
    key = jax.random.key(seed)
    ks = jax.random.split(key, 20)
    f32 = jnp.float32
    wb = min(WINDOW, PAST_LEN)

    def nrm(k, shape, scale):
        return scale * jax.random.normal(k, shape, f32)

    def gain(k, shape):
        return 1.0 + 0.05 * jax.random.normal(k, shape, f32)

    return {
        "x_prompt": nrm(ks[0], (BATCH, SEQ, D_MODEL), 1.0),
        "x_sample": nrm(ks[1], (DEC_BATCH, DEC_SEQ, D_MODEL), 1.0),
        "cache_k": nrm(ks[2], (DEPTH, DEC_BATCH, wb, N_HEADS_A, HEAD_DIM_A), 1.0),
        "cache_v": nrm(ks[3], (DEPTH, DEC_BATCH, wb, N_HEADS_A, HEAD_DIM_A), 1.0),
        "state_hgrn": nrm(ks[4], (DEPTH, DEC_BATCH, N_HEADS_B, DK_B, DV_B), 0.3),
        "norm_ffn1": gain(ks[5], (DEPTH, D_MODEL)),
        "ffn1_w_gate_up": nrm(ks[6], (DEPTH, D_MODEL, 2 * D_FF), D_MODEL ** -0.5),
        "ffn1_w_down": nrm(ks[7], (DEPTH, D_FF, D_MODEL), D_FF ** -0.5),
        "norm_mix": gain(ks[8], (DEPTH, D_MODEL)),
        "w_in": nrm(ks[9], (DEPTH, D_MODEL, IN_COLS), D_MODEL ** -0.5),
        "q_norm": gain(ks[10], (DEPTH, HEAD_DIM_A)),
        "k_norm": gain(ks[11], (DEPTH, HEAD_DIM_A)),
        "gamma_lb": nrm(ks[12], (DEPTH + 1, HK_B), 0.5),
        "attn_out_norm": gain(ks[13], (DEPTH, ATTN_WIDTH)),
        "hgrn_out_norm": gain(ks[14], (DEPTH, HGRN_WIDTH)),
        "w_out": nrm(ks[15], (DEPTH, MIX_WIDTH, D_MODEL), MIX_WIDTH ** -0.5),
        "norm_ffn2": gain(ks[16], (DEPTH, D_MODEL)),
        "ffn2_w_gate_up": nrm(ks[17], (DEPTH, D_MODEL, 2 * D_FF), D_MODEL ** -0.5),
        "ffn2_w_down": nrm(ks[18], (DEPTH, D_FF, D_MODEL), D_FF ** -0.5),
    }


def reference(x_prompt, x_sample, cache_k, cache_v, state_hgrn, norm_ffn1, ffn1_w_gate_up,
              ffn1_w_down, norm_mix, w_in, q_norm, k_norm, gamma_lb, attn_out_norm,
              hgrn_out_norm, w_out, norm_ffn2, ffn2_w_gate_up, ffn2_w_down):
    lb_all = jnp.cumsum(jax.nn.softmax(gamma_lb.astype(jnp.float32), axis=0), axis=0)
    hp, hs = x_prompt, x_sample
    kp_l, vp_l, sp_l, ks_l, vs_l, ss_l = [], [], [], [], [], []
    for l in range(DEPTH):
        mp = (w_in[l], q_norm[l], k_norm[l], lb_all[l], attn_out_norm[l], hgrn_out_norm[l], w_out[l])
        hp = hp + 0.5 * swiglu(rms_norm(hp, norm_ffn1[l]), ffn1_w_gate_up[l], ffn1_w_down[l])
        hs = hs + 0.5 * swiglu(rms_norm(hs, norm_ffn1[l]), ffn1_w_gate_up[l], ffn1_w_down[l])
        yp, kp, vp, sp = mixer_prompt(rms_norm(hp, norm_mix[l]), *mp)
        ys, kn, vn, sn = mixer_sample(rms_norm(hs, norm_mix[l]), cache_k[l], cache_v[l], state_hgrn[l], *mp)
        hp = hp + yp
        hs = hs + ys
        hp = hp + 0.5 * swiglu(rms_norm(hp, norm_ffn2[l]), ffn2_w_gate_up[l], ffn2_w_down[l])
        hs = hs + 0.5 * swiglu(rms_norm(hs, norm_ffn2[l]), ffn2_w_gate_up[l], ffn2_w_down[l])
        kp_l.append(kp); vp_l.append(vp); sp_l.append(sp)
        ks_l.append(kn); vs_l.append(vn); ss_l.append(sn)
    return (hp, hs, jnp.stack(kp_l), jnp.stack(vp_l), jnp.stack(sp_l),
            jnp.stack(ks_l), jnp.stack(vs_l), jnp.stack(ss_l))
```

```python
from contextlib import ExitStack
import numpy as np
import concourse.bass as bass
import concourse.mybir as mybir
from concourse.bass_utils import run_bass_kernel_spmd

F32 = mybir.dt.float32
BF16 = mybir.dt.bfloat16
I32 = mybir.dt.int32
AF = mybir.ActivationFunctionType
ALU = mybir.AluOpType
AX = mybir.AxisListType

D = 1024
DFF = 2816
NCORE = 8
SLICE = 2048
NSL = 5
GT = 512
NG = NSL * SLICE // GT
OWN0 = NG - 4
HALO0 = NG - 8
EPS = 1e-6
NKS = 20
RING = 10


class Eng:
    def __init__(self, name, sem):
        self.name = name
        self.sem = sem
        self.count = 0
        self.prog = []
        self.seen = {}


class Buf:
    def __init__(self, name=""):
        self.name = name
        self.w = []
        self.r = {}
        self.dsem = None
        self.dcount = 0


class Builder:
    def __init__(self, nc, es):
        self.nc = nc
        self.es = es
        self.nsem = 0
        self.dbufs = []
        self.rec = None

    def sem(self, name):
        self.nsem += 1
        return self.es.enter_context(self.nc.semaphore(f"{name}_{self.nsem}"))

    def _waits(self, eng, reads, writes):
        waits = {}

        def need(sv):
            sem, val = sv
            if waits.get(id(sem), (None, 0))[1] < val:
                waits[id(sem)] = (sem, val)
        for b in reads:
            for sv in b.w:
                need(sv)
        for b in writes:
            for sv in b.w:
                need(sv)
            for sv in b.r.values():
                need(sv)
        wl = []
        for sem, val in waits.values():
            if eng.seen.get(id(sem), 0) >= val:
                continue
            eng.seen[id(sem)] = val
            wl.append((sem, val))
        return wl

    def op(self, eng, fn, reads=(), writes=()):
        if self.rec is not None:
            rec, self_ = self.rec, self
            reads, writes = list(reads), list(writes)

            def thunk():
                saved, self_.rec = self_.rec, None
                self_.op(eng, fn, reads, writes)
                self_.rec = saved
            rec.append(thunk)
            return
        wl = self._waits(eng, reads, writes)
        eng.count += 1
        me = (eng.sem, eng.count)
        eng.prog.append((wl, fn, eng.sem, 1))
        for b in writes:
            b.w = [me]
            b.r = {}
        for b in reads:
            b.r[id(eng.sem)] = me

    def raw(self, eng, fn, reads=(), writes=(), dma_bufs=()):
        wl = self._waits(eng, reads, writes)
        for b in dma_bufs:
            if b.dsem is None:
                b.dsem = self.sem("d")
                self.dbufs.append(b)
            b.dcount += 1
            b.w = [(b.dsem, 16 * b.dcount)]
            b.r = {}
        eng.prog.append((wl, fn, None, 0))

    def dma_multi(self, q, pairs, reads=(), writes=(), sb=None):
        wl = self._waits(q, reads, writes)
        if sb.dsem is None:
            sb.dsem = self.sem("d")
            self.dbufs.append(sb)
        for i, (out, in_) in enumerate(pairs):
            def fn(e, out=out, in_=in_):
                return e.dma_start(out=out, in_=in_, allow_slow_non_contiguous=True)
            q.prog.append((wl if i == 0 else [], fn, sb.dsem, 16))
        sb.dcount += len(pairs)
        me = (sb.dsem, 16 * sb.dcount)
        for b in writes:
            b.w = [me]
            b.r = {}
        for b in reads:
            b.r[id(sb.dsem)] = me

    def dma(self, q, out, in_, reads=(), writes=(), sb=None):
        wl = self._waits(q, reads, writes)
        if sb.dsem is None:
            sb.dsem = self.sem("d")
            self.dbufs.append(sb)
        sb.dcount += 1
        me = (sb.dsem, 16 * sb.dcount)

        def fn(e, out=out, in_=in_):
            o = out(e) if callable(out) else out
            i = in_(e) if callable(in_) else in_
            return e.dma_start(out=o, in_=i, allow_slow_non_contiguous=True)
        q.prog.append((wl, fn, sb.dsem, 16))
        for b in writes:
            b.w = [me]
            b.r = {}
        for b in reads:
            b.r[id(sb.dsem)] = me


def build_program():
    nc = bass.Bass("TRN2", target_bir_lowering=False)
    es = ExitStack()
    K = Builder(nc, es)

    def din(name, shape, dt=F32):
        return nc.dram_tensor(name, list(shape), dt, kind="ExternalInput").ap()

    def dout(name, shape):
        return nc.dram_tensor(name, list(shape), F32, kind="ExternalOutput").ap()

    def dtmp(name, shape, dt):
        return nc.dram_tensor(name, list(shape), dt, kind="Internal").ap()

    def sb(name, shape, dt=F32):
        return es.enter_context(nc.sbuf_tensor(name, list(shape), dt))

    def ps(name, shape, dt=F32):
        return es.enter_context(nc.psum_tensor(name, list(shape), dt))

    xp = din("xp", [NSL * SLICE, D])
    xs = din("xs", [64, D])
    ck = din("ck", [16, 2048, 512])
    cv = din("cv", [16, 2048, 512])
    st_in = din("st", [16, 4, 128, 128])
    w_gu = [din("w_gu1", [D, 2 * DFF]), din("w_gu2", [D, 2 * DFF])]
    w_dn = [din("w_d1", [DFF, D]), din("w_d2", [DFF, D])]
    w_in = din("w_in", [D, 3584])
    w_out = din("w_out", [D, D])
    g_n = [din("g1", [1, D]), din("gm", [1, D]), din("g2", [1, D])]
    g_q = din("gq", [1, 64])
    g_k = din("gk", [1, 64])
    g_lb = din("glb", [2, 512])
    g_ao = din("gao", [1, 512])
    g_ho = din("gho", [1, 512])
    c_mult = din("c_mult", [128, 17 * 128])
    c_ab = din("c_ab", [128, 8 * 17])
    c_ws = din("c_ws", [128, 224])
    c_wn = din("c_wn", [64, 16 * 32])
    c_tri = din("c_tri", [64, 64])
    c_id = din("c_id", [128, 128])
    c_flag = din("c_flag", [128, 1])
    c_gfl = din("c_gfl", [128, 16])
    c_role = din("c_role", [128, 2])
    tok = din("tok", [1, 16], I32)
    needx = din("needx", [1, 16], I32)
    zer = din("zer", [128, 512])
    SH_S2 = nc.dram_tensor("sh_s", [2 * 2 * 128, 512], F32, kind="Internal", addr_space="Shared").ap()
    SH_L2 = nc.dram_tensor("sh_l", [2 * 128, 8], F32, kind="Internal", addr_space="Shared").ap()
    SH_F = nc.dram_tensor("sh_f", [2, 16], I32, kind="Internal", addr_space="Shared").ap()

    y_p = dout("y_p", [SLICE, D])
    y_s = dout("y_s", [64, D])
    k_p = dout("k_p", [SLICE, 512])
    v_p = dout("v_p", [SLICE, 512])
    st_p = dout("st_p", [4, 128, 128])
    k_s = dout("k_s", [64, 512])
    v_s = dout("v_s", [64, 512])
    st_s = dout("st_s", [16, 4, 128, 128])

    S_GU = [dtmp(f"s_gu{i}", [22, 2, 128, 8, 128], BF16) for i in range(2)]
    S_DN = [dtmp(f"s_dn{i}", [22, 128, D], BF16) for i in range(2)]
    S_INL = dtmp("s_inl", [28, 128, 8, 128], BF16)
    S_INR = dtmp("s_inr", [8, 128, 3584], BF16)
    S_OUT = dtmp("s_out", [8, 128, D], BF16)
    S_AO = dtmp("s_ao", [64, 8 * 65], F32)
    scratchB = Buf("scratch")
    scrB = {}
    aoB = Buf("ao_scratch")

    PE = Eng("pe", K.sem("pe"))
    ACT = Eng("act", K.sem("act"))
    DVE = Eng("dve", K.sem("dve"))
    POOL = Eng("pool", K.sem("pool"))
    SP = Eng("sp", K.sem("sp"))

    H = [sb(f"H{i}", [128, 4, D]) for i in range(2)]
    HB = [Buf(f"H{i}") for i in range(2)]
    XT = sb("XT", [128, 8, GT], BF16)
    XTB = Buf("XT")
    XN = sb("XN", [128, D], BF16)
    XNB = Buf("XN")
    XNB2 = [Buf("XNa"), Buf("XNb")]
    ACTT = sb("ACTT", [128, 22, GT], BF16)
    ACTTB = [Buf(f"actt{j}") for j in range(22)]
    RNG = sb("RNG", [128, RING, 1024], BF16)
    RNGB = [Buf(f"ring{i}") for i in range(RING)]
    TMPA = [sb(f"TMPA{i}", [128, GT]) for i in range(3)]
    TMPAB = [Buf(f"tmpa{i}") for i in range(3)]
    KT = sb("KT", [128, 4, NKS * 128], BF16)
    KTB = Buf("KT")
    VPraw = sb("VP", [128, NKS * 8 * 65], BF16)
    VP = VPraw[:].rearrange("p (a h e) -> p a h e", a=NKS, h=8)
    VPB = Buf("VP")
    QT = sb("QT", [128, 4, GT], BF16)
    QTB = Buf("QT")
    CATT = sb("CATT", [128, 8, GT], BF16)
    CATB = [Buf(f"cat{i}") for i in range(8)]
    MULT = sb("MULT", [128, 17, 128], BF16)
    AB = sb("AB", [128, 8, 17])
    WS = sb("WS", [128, 224])
    WN = sb("WN", [64, 16, 32])
    TRI = sb("TRI", [64, 64])
    IDB = sb("IDB", [128, 128], BF16)
    ONES = sb("ONES", [128, 128])
    FLAG = sb("FLAG", [128, 1])
    GFL = sb("GFL", [128, 16])
    ROLE = sb("ROLE", [128, 2])
    LDACC = sb("LDACC", [128, 4])
    LDT = sb("LDT", [128, 4])
    LDM = sb("LDM", [128, 8])
    LDP = sb("LDP", [128, 8])
    ldB = Buf("ldacc")
    ldtB = Buf("ldt")
    ldmB = Buf("ldm")
    ldpB = Buf("ldp")
    EPST = sb("EPST", [128, 1])
    GT3 = sb("GT3", [128, 3, 8])
    GQ = sb("GQ", [128, 64])
    GK = sb("GK", [128, 64])
    GAOT = sb("GAOT", [128, 4])
    GHO = sb("GHO", [128, 4])
    LBT = sb("LBT", [128, 2, 4])
    LB = sb("LB", [128, 4])
    OML = sb("OML", [128, 4])
    NOML = sb("NOML", [128, 4])
    constB = Buf("const")
    QKV = [sb(f"QKV{i}", [128, 512]) for i in range(3)]
    QKVB = [Buf(f"qkv{i}") for i in range(3)]
    QB16 = [sb(f"QB16_{i}", [128, 512], BF16) for i in range(2)]
    QB16B = [Buf(f"qb16{i}") for i in range(2)]
    SM = [sb(f"SM{i}", [128, 16]) for i in range(6)]
    SMB = [Buf(f"sm{i}") for i in range(6)]
    PT = [sb(f"PT{i}", [128, GT], BF16) for i in range(3)]
    PTB = [Buf(f"pt{i}") for i in range(3)]
    AO = sb("AO", [128, 4, 8, 65])
    AOB = [Buf(f"ao{i}") for i in range(4)]
    VH = sb("VH", [64, 8, 512], BF16)
    VHB = [Buf(f"vh{i}") for i in range(8)]
    HT = [sb(f"HT{i}", [128, GT]) for i in range(8)]
    HTB = [Buf(f"ht{i}") for i in range(8)]
    QTIL = sb("QTIL", [128, GT], BF16)
    QTILB = Buf("qtil")
    KTIL = sb("KTIL", [128, GT], BF16)
    KTILB = Buf("ktil")
    KTOK = [sb(f"KTOK{i}", [64, 128], BF16) for i in range(3)]
    KTOKB = [Buf(f"ktok{i}") for i in range(3)]
    ATM = [sb(f"ATM{i}", [64, 64], BF16) for i in range(3)]
    ATMB = [Buf(f"atm{i}") for i in range(3)]
    SPE = [sb(f"SPE{i}", [128, 128]) for i in range(3)]
    SPEB = [Buf(f"spe{i}") for i in range(3)]
    SBF = [sb(f"SBF{i}", [128, 128], BF16) for i in range(3)]
    SBFB = [Buf(f"sbf{i}") for i in range(3)]
    ST4 = sb("ST4", [128, 4, 128])
    STATE = [ST4[:, i, :] for i in range(4)]
    STATEB = [Buf(f"state{i}") for i in range(4)]
    SST = [sb(f"SST{i}", [128, 128]) for i in range(3)]
    SSTB = [Buf(f"sst{i}") for i in range(3)]
    CST, CSTB = HT, HTB
    CB16, CB16B = QB16, QB16B

    PS = [ps(f"PS{i}", [128, 512]) for i in range(6)]
    PSB = [Buf(f"ps{i}") for i in range(6)]
    PSH = [ps(f"PSH{i}", [128, 1024], BF16) for i in range(2)]
    PSHB = [Buf(f"psh{i}") for i in range(2)]
    rr = {"ps6": 0, "p2b": 0, "psa": 0, "ps": 0, "psh": 0, "ring": 0, "tmpa": 0, "qkv": 0, "qb16": 0, "sm": 0, "pt": 0, "ht": 0,
          "ktok": 0, "atm": 0, "spe": 0, "sbf": 0, "cst": 0, "cb16": 0, "sst": 0}

    def nxt(key, n):
        i = rr[key]
        rr[key] = (i + 1) % n
        return i

    def psum():
        i = nxt("ps", 4)
        return PS[i], PSB[i]

    def psum6():
        i = nxt("ps6", 6)
        return PS[i], PSB[i]

    def psum_acc():
        i = 4 + nxt("psa", 2)
        return PS[i], PSB[i]

    def psumh():
        i = nxt("psh", 2)
        return PSH[i], PSHB[i]

    def MM(items, reads, writes):
        def fn(e, items=items):
            last = None
            for (o, l, r, st, sp_) in items:
                last = e.matmul(o, l, r, start=st, stop=sp_, skip_group_check=True)
            return last
        K.op(PE, fn, reads, writes)

    def TR(items, reads, writes):
        def fn(e, items=items):
            last = None
            for (o, i) in items:
                last = e.transpose(o, i, IDB[0:i.shape[0], 0:i.shape[0]])
            return last
        K.op(PE, fn, reads + [constB], writes)

    def A(out, in_, func, reads, writes, bias=None, scale=None, accum_out=None):
        def fn(e):
            kw = {}
            if bias is not None:
                kw["bias"] = bias
            if scale is not None:
                kw["scale"] = scale
            if accum_out is not None:
                kw["accum_out"] = accum_out
            return e.activation(out, in_, func, **kw)
        K.op(ACT, fn, reads, writes)

    def TT(eng, out, in0, in1, op, reads, writes):
        K.op(eng, lambda e: e.tensor_tensor(out, in0, in1, op), reads, writes)

    def TS(eng, out, in0, s1, s2, op0, op1, reads, writes):
        if op1 is None:
            K.op(eng, lambda e: e.tensor_scalar(out, in0, s1, None, op0), reads, writes)
        else:
            K.op(eng, lambda e: e.tensor_scalar(out, in0, s1, s2, op0, op1), reads, writes)

    def STT(out, in0, scalar, in1, op0, op1, reads, writes):
        K.op(DVE, lambda e: e.scalar_tensor_tensor(out, in0, scalar, in1, op0, op1), reads, writes)

    def CP(eng, out, in_, reads, writes):
        if eng is ACT:
            K.op(eng, lambda e: e.copy(out, in_), reads, writes)
        else:
            K.op(eng, lambda e: e.tensor_copy(out, in_), reads, writes)

    def RECIP(out, in_, reads, writes):
        K.op(DVE, lambda e: e.reciprocal(out, in_), reads, writes)

    def LOAD(out, in_, b, reads=()):
        K.dma(SP, out, in_, reads=list(reads), writes=[b], sb=b)

    def STORE(out, in_, b, writes=()):
        K.dma(POOL, out, in_, reads=[b], writes=list(writes), sb=b)

    cb = constB
    LOAD(AB[:].rearrange("p h d -> p (h d)"), c_ab, cb)
    LOAD(WS[:], c_ws, cb)
    LOAD(WN[:].rearrange("p b c -> p (b c)"), c_wn, cb)
    LOAD(TRI[:], c_tri, cb)
    LOAD(FLAG[:], c_flag, cb)
    LOAD(GFL[:], c_gfl, cb)
    LOAD(ROLE[:], c_role, cb)
    K.op(DVE, lambda e: e.memset(LDACC[:], 0.0), [], [ldB])
    for i in range(3):
        LOAD(GT3[:, i, :], g_n[i].rearrange("o (k p) -> p (o k)", p=128), cb)
    LOAD(GAOT[:], g_ao.rearrange("o (k p) -> p (o k)", p=128), cb)
    LOAD(GHO[:], g_ho.rearrange("o (h p) -> p (o h)", p=128), cb)
    LOAD(LBT[:], g_lb.rearrange("r (h p) -> p r h", p=128), cb)
    LOAD(GQ[:, :], g_q[0:1, :].to_broadcast([128, 64]), cb)
    LOAD(GK[:, :], g_k[0:1, :].to_broadcast([128, 64]), cb)
    LOAD(H[0][:, 0, 0:128], c_id, HB[0])
    LOAD(H[0][:, 1:4, :].rearrange("p a d -> p (a d)")[:, 0:2176], c_mult, HB[0])
    CP(DVE, IDB[:], H[0][:, 0, 0:128], [HB[0]], [cb])
    CP(DVE, MULT[:].rearrange("p a b -> p (a b)"), H[0][:, 1:4, :].rearrange("p a d -> p (a d)")[:, 0:2176], [HB[0]], [cb])
    K.op(DVE, lambda e: e.memset(ONES[:], 1.0), [], [cb])
    K.op(DVE, lambda e: e.memset(EPST[:], EPS), [], [cb])
    TS(DVE, GQ[:, :], GQ[:, :], 0.125, None, ALU.mult, None, [cb], [cb])
    TT(DVE, LB[:], LBT[:, 0, :], LBT[:, 1, :], ALU.subtract, [cb], [cb])
    A(OML[:], LB[:], AF.Sigmoid, [cb], [cb], scale=-1.0)
    A(LB[:], LB[:], AF.Sigmoid, [cb], [cb])
    TS(DVE, NOML[:], OML[:], -1.0, None, ALU.mult, None, [cb], [cb])

    stg_i = [0]

    pcB = [Buf(f"pc{i}") for i in range(4)]
    pc_i = [0]
    pc_thunks = []

    def pc_dma(dst, src):
        def th(dst=dst, src=src):
            b = pcB[pc_i[0] % 4]
            pc_i[0] += 1
            K.dma(POOL, dst, src, reads=[], writes=[], sb=b)
        pc_thunks.append(th)

    def pc_mark(name):
        def th():
            b = Buf(name)
            b.w = [(x.dsem, 16 * x.dcount) for x in pcB if x.dsem is not None]
            scrB[name] = b
        pc_thunks.append(th)

    def precast(W, Kdim, N, name, dst_row=None, dst_lhs=None):
        nk = Kdim // 128
        if N == 2 * DFF:
            for j0 in (0, 11):
                for kc in range(nk):
                    rows = W[kc * 128:(kc + 1) * 128, :]
                    for gu in range(2):
                        pc_dma(dst_lhs[j0:j0 + 11, gu, :, kc, :].rearrange("j p c -> p j c"),
                               rows[:, gu * DFF + j0 * 128:gu * DFF + (j0 + 11) * 128].rearrange("p (j c) -> p j c", c=128))
                pc_mark(name + ("a" if j0 == 0 else ""))
            return
        for kc in range(nk):
            rows = W[kc * 128:(kc + 1) * 128, :]
            if dst_row is not None:
                for c0 in range(0, N, 1792):
                    w_ = min(1792, N - c0)
                    pc_dma(dst_row[kc, :, c0:c0 + w_], rows[:, c0:c0 + w_])
            if dst_lhs is not None:
                if N == 2 * DFF:
                    for gu in range(2):
                        for j0 in (0, 11):
                            pc_dma(dst_lhs[j0:j0 + 11, gu, :, kc, :].rearrange("j p c -> p j c"),
                                   rows[:, gu * DFF + j0 * 128:gu * DFF + (j0 + 11) * 128].rearrange("p (j c) -> p j c", c=128))
                else:
                    for j0 in (0, 14):
                        pc_dma(dst_lhs[j0:j0 + 14, :, kc, :].rearrange("j p c -> p j c"),
                               rows[:, j0 * 128:(j0 + 14) * 128].rearrange("p (j c) -> p j c", c=128))
        pc_mark(name)

    precast(w_gu[0], D, 2 * DFF, "gu0", dst_lhs=S_GU[0])
    precast(w_dn[0], DFF, D, "dn0", dst_row=S_DN[0])
    precast(w_in, D, 3584, "in", dst_row=S_INR, dst_lhs=S_INL)
    precast(w_out, D, D, "out", dst_row=S_OUT)
    precast(w_gu[1], D, 2 * DFF, "gu1", dst_lhs=S_GU[1])
    precast(w_dn[1], DFF, D, "dn1", dst_row=S_DN[1])
    for t_ in pc_thunks:
        t_()
    pc_rest = []

    def pc_need(name):
        assert name in scrB

    def stg_final():
        return {}

    def wload(dram_ap, ncols, view=None, scr="in"):
        i = nxt("ring", RING)
        dst = RNG[:, i, 0:ncols]
        if view is not None:
            dst = view(dst)
        LOAD(dst, dram_ap, RNGB[i], reads=[scrB[scr]])
        return RNG[:, i, :], RNGB[i]

    def load_x(hi, src_rows, T):
        nt = max(T // 128, 1)
        if T >= 128:
            LOAD(H[hi][:, 0:nt, :], src_rows.rearrange("(t p) d -> p t d", p=128), HB[hi])
        else:
            LOAD(H[hi][0:T, 0, :], src_rows, HB[hi])

    def norm_T(hi, T, gi):
        norm_apply(hi, T, gi, norm_stats(hi, T))

    def norm_stats(hi, T):
        nt = max(T // 128, 1)
        P = min(T, 128)
        si = nxt("sm", 6)
        for tt in range(nt):
            ti = nxt("tmpa", 3)
            A(TMPA[ti][0:P, :], H[hi][0:P, tt, 0:512], AF.Square, [HB[hi]], [TMPAB[ti], SMB[si]], accum_out=SM[si][0:P, tt:tt + 1])
            A(TMPA[ti][0:P, :], H[hi][0:P, tt, 512:1024], AF.Square, [HB[hi]], [TMPAB[ti], SMB[si]], accum_out=SM[si][0:P, 4 + tt:5 + tt])
        TT(DVE, SM[si][0:P, 8:8 + nt], SM[si][0:P, 0:nt], SM[si][0:P, 4:4 + nt], ALU.add, [SMB[si]], [SMB[si]])
        A(SM[si][0:P, 8:8 + nt], SM[si][0:P, 8:8 + nt], AF.Sqrt, [SMB[si], cb], [SMB[si]], bias=EPST[0:P, 0:1], scale=1.0 / D)
        RECIP(SM[si][0:P, 12:12 + nt], SM[si][0:P, 8:8 + nt], [SMB[si]], [SMB[si]])
        return si

    def norm_apply(hi, T, gi, sis):
        nt = max(T // 128, 1)
        for tt in range(nt):
            P = min(T, 128)
            si = sis
            pt, pb = psumh()
            for hf in range(2):
                A(XN[0:P, hf * 512:(hf + 1) * 512], H[hi][0:P, tt, hf * 512:(hf + 1) * 512], AF.Copy, [HB[hi], SMB[si]], [XNB2[hf]],
                  scale=SM[si][0:P, 12 + tt:13 + tt])
                TR([(pt[:, kc * 128:kc * 128 + P], XN[0:P, kc * 128:(kc + 1) * 128]) for kc in range(4 * hf, 4 * hf + 4)], [XNB2[hf]], [pb])
            TT(DVE, XT[:, :, tt * 128:tt * 128 + P], pt[:, :].rearrange("p (k c) -> p k c", c=128)[:, :, 0:P],
               GT3[:, gi, :].unsqueeze(2).to_broadcast([128, 8, P]), ALU.mult, [pb, cb], [XTB])

    def ffn(hi, T, wi, prefetch=None):
        ffn_gateup(hi, T, wi)
        ffn_down(hi, T, wi, prefetch)

    def ffn_gateup(hi, T, wi, drain=None):
        for j in range(22):
            scr_ = f"gu{wi}a" if j < 11 else f"gu{wi}"
            wg, wgb = wload(S_GU[wi][j, 0], 1024, view=lambda d: d.rearrange("p (k c) -> p k c", c=128), scr=scr_)
            wu, wub = wload(S_GU[wi][j, 1], 1024, view=lambda d: d.rearrange("p (k c) -> p k c", c=128), scr=scr_)
            pg, pgb = psum6()
            pu, pub = psum6()
            MM([(pg[:, 0:T], wg[:, kc * 128:(kc + 1) * 128], XT[:, kc, 0:T], kc == 0, kc == 7) for kc in range(8)], [wgb, XTB], [pgb])
            MM([(pu[:, 0:T], wu[:, kc * 128:(kc + 1) * 128], XT[:, kc, 0:T], kc == 0, kc == 7) for kc in range(8)], [wub, XTB], [pub])
            ti = nxt("tmpa", 3)
            A(TMPA[ti][:, 0:T], pg[:, 0:T], AF.Silu, [pgb], [TMPAB[ti]])
            TT(DVE, ACTT[:, j, 0:T], TMPA[ti][:, 0:T], pu[:, 0:T], ALU.mult, [TMPAB[ti], pub], [ACTTB[j]])
            if drain:
                for _ in range(3):
                    if drain:
                        drain.pop(0)()
            for _ in range(3):
                if pc_rest:
                    pc_rest.pop(0)()
        while drain:
            drain.pop(0)()

    def ffn_down(hi, T, wi, prefetch=None):
        nt = max(T // 128, 1)
        P = min(T, 128)
        if prefetch is not None:
            prefetch()
        for hf in range(2):
            pacc = [psum() for _ in range(nt)]
            for j in range(22):
                wd, wdb = wload(S_DN[wi][j, :, hf * 512:(hf + 1) * 512], 512, scr=f"dn{wi}")
                MM([(pacc[tt][0][0:P, :], ACTT[:, j, tt * 128:tt * 128 + P], wd[:, 0:512], j == 0, j == 21) for tt in range(nt)],
                   [wdb, ACTTB[j]], [pacc[tt][1] for tt in range(nt)])
            for tt in range(nt):
                STT(H[hi][0:P, tt, hf * 512:(hf + 1) * 512], pacc[tt][0][0:P, :], 0.5, H[hi][0:P, tt, hf * 512:(hf + 1) * 512],
                    ALU.mult, ALU.add, [pacc[tt][1], HB[hi]], [HB[hi]])

    def proj_feat(ct, T, t0=0):
        w, wb = wload(S_INL[ct], 1024, view=lambda d: d.rearrange("p (k c) -> p k c", c=128))
        p, pb = psum()
        MM([(p[:, 0:T], w[:, kc * 128:(kc + 1) * 128], XT[:, kc, t0:t0 + T], kc == 0, kc == 7) for kc in range(8)], [wb, XTB], [pb])
        return p, pb

    def proj_tok_weights(c0):
        return [wload(S_INR[kc, :, c0:c0 + 512], 512) for kc in range(8)]

    def proj_tok(ws, cols, M):
        p, pb = psum()
        MM([(p[0:M, :], XT[:, kc, cols[0]:cols[1]], ws[kc][0][:, 0:512], kc == 0, kc == 7) for kc in range(8)],
           [w[1] for w in ws] + [XTB], [pb])
        return p, pb

    SMQ = sb("SMQ", [128, 64])
    smqB = Buf("smq")

    def qk_stats(ps_, P):
        n_ = len(ps_)
        for tt, (p, pb) in enumerate(ps_):
            ti = nxt("tmpa", 3)
            A(TMPA[ti][0:P, :], p[0:P, :], AF.Square, [pb], [TMPAB[ti]])
            K.op(DVE, lambda e, ti=ti, tt=tt: e.tensor_reduce(SMQ[0:P, 8 * tt:8 * tt + 8], TMPA[ti][0:P, :].rearrange("p (h e) -> p h e", e=64), AX.X, ALU.add),
                 [TMPAB[ti]], [smqB])
        A(SMQ[0:P, 32:32 + 8 * n_], SMQ[0:P, 0:8 * n_], AF.Sqrt, [smqB, cb], [smqB], bias=EPST[0:P, 0:1], scale=1.0 / 64)
        RECIP(SMQ[0:P, 0:8 * n_], SMQ[0:P, 32:32 + 8 * n_], [smqB], [smqB])

    def qknorm(p, pb, P, G, out32=None, out32b=None, tt=0):
        ti = nxt("tmpa", 3)
        TT(DVE, TMPA[ti][0:P, :].rearrange("p (h e) -> p h e", e=64), G[0:P, :].unsqueeze(1).to_broadcast([P, 8, 64]),
           SMQ[0:P, 8 * tt:8 * tt + 8].unsqueeze(2).to_broadcast([P, 8, 64]), ALU.mult, [smqB, cb], [TMPAB[ti]])
        bi = nxt("qb16", 2)
        if out32 is not None:
            TT(DVE, out32[0:P, :], p[0:P, :], TMPA[ti][0:P, :], ALU.mult, [pb, TMPAB[ti]], [out32b])
            CP(ACT, QB16[bi][0:P, :], out32[0:P, :], [out32b], [QB16B[bi]])
        else:
            TT(DVE, QB16[bi][0:P, :], p[0:P, :], TMPA[ti][0:P, :], ALU.mult, [pb, TMPAB[ti]], [QB16B[bi]])
        return bi

    def tr_pairs(bi, P, dst, dstb):
        pt, ptb = psumh()
        TR([(pt[:, hp * 128:hp * 128 + P], QB16[bi][0:P, hp * 128:(hp + 1) * 128]) for hp in range(4)], [QB16B[bi]], [ptb])
        CP(ACT, dst, pt[:, 0:512].rearrange("p (k c) -> p k c", c=128)[:, :, 0:P], [ptb], [dstb])

    def attn_kv(T, gtile0, kout=None, vout=None, kvrows=None, halo=False):
        nt = max(T // 128, 1)
        P = min(T, 128)
        wk = proj_tok_weights(512)
        pk = [proj_tok(wk, (tt * 128, tt * 128 + P), P) for tt in range(nt)]
        qk_stats(pk, P)
        for tt in range(nt):
            slot = (gtile0 + tt) % NKS
            qi = nxt("qkv", 3)
            bi = qknorm(pk[tt][0], pk[tt][1], P, GK, out32=QKV[qi], out32b=QKVB[qi], tt=tt)
            if kout is not None:
                STORE(kout[kvrows + tt * 128:kvrows + tt * 128 + P, :], QKV[qi][0:P, :], QKVB[qi])
            tr_pairs(bi, P, KT[:, :, slot * 128:slot * 128 + P], KTB)
        wv = proj_tok_weights(1024)
        pv = [proj_tok(wv, (tt * 128, tt * 128 + P), P) for tt in range(nt)]
        for tt in range(nt):
            slot = (gtile0 + tt) % NKS
            qi = nxt("qkv", 3)
            CP(ACT, QKV[qi][0:P, :], pv[tt][0][0:P, :], [pv[tt][1]], [QKVB[qi]])
            if vout is not None:
                STORE(vout[kvrows + tt * 128:kvrows + tt * 128 + P, :], QKV[qi][0:P, :], QKVB[qi])
            CP(DVE, VP[0:P, slot, :, 0:64], QKV[qi][0:P, :].rearrange("p (h e) -> p h e", e=64), [QKVB[qi]], [VPB])
            if halo:
                CP(DVE, VP[0:P, slot, :, 64:65], FLAG[0:P, 0:1].unsqueeze(1).to_broadcast([P, 8, 1]), [cb], [VPB])
            else:
                K.op(DVE, lambda e, slot=slot: e.memset(VP[0:P, slot, :, 64:65], 1.0), [], [VPB])

    def attn_q(T):
        nt = max(T // 128, 1)
        P = min(T, 128)
        wq = proj_tok_weights(0)
        pq = [proj_tok(wq, (tt * 128, tt * 128 + P), P) for tt in range(nt)]
        qk_stats(pq, P)
        for tt in range(nt):
            bi = qknorm(pq[tt][0], pq[tt][1], P, GQ, tt=tt)
            tr_pairs(bi, P, QT[:, :, tt * 128:tt * 128 + P], QTB)

    def attn_finish(P, aoi, tcol):
        si = nxt("sm", 6)
        ti = nxt("tmpa", 3)
        RECIP(SM[si][0:P, 0:8], AO[0:P, aoi, :, 64], [AOB[aoi]], [SMB[si]])
        TT(DVE, TMPA[ti][0:P, :].rearrange("p (h e) -> p h e", e=64), AO[0:P, aoi, :, 0:64],
           SM[si][0:P, 0:8].unsqueeze(2).to_broadcast([P, 8, 64]), ALU.mult, [AOB[aoi], SMB[si]], [TMPAB[ti]])
        t2 = nxt("tmpa", 3)
        A(TMPA[t2][0:P, :], TMPA[ti][0:P, :], AF.Square, [TMPAB[ti]], [TMPAB[t2], SMB[si]], accum_out=SM[si][0:P, 8:9])
        A(SM[si][0:P, 9:10], SM[si][0:P, 8:9], AF.Sqrt, [SMB[si], cb], [SMB[si]], bias=EPST[0:P, 0:1], scale=1.0 / 512)
        RECIP(SM[si][0:P, 10:11], SM[si][0:P, 9:10], [SMB[si]], [SMB[si]])
        bi = nxt("qb16", 2)
        TS(DVE, QB16[bi][0:P, :], TMPA[ti][0:P, :], SM[si][0:P, 10:11], None, ALU.mult, None, [TMPAB[ti], SMB[si]], [QB16B[bi]])
        pt, ptb = psumh()
        TR([(pt[:, hp * 128:hp * 128 + P], QB16[bi][0:P, hp * 128:(hp + 1) * 128]) for hp in range(4)], [QB16B[bi]], [ptb])
        TT(DVE, CATT[:, 0:4, tcol:tcol + P], pt[:, 0:512].rearrange("p (k c) -> p k c", c=128)[:, :, 0:P],
           GAOT[:, :].unsqueeze(2).to_broadcast([128, 4, P]), ALU.mult, [ptb, cb], CATB[0:4])

    def attn_prompt(gtile0):
        halo_lim = HALO0 * 4 + 16
        for h in range(8):
            hp, base = h // 2, 64 * (h % 2)
            po, pob = psum_acc()
            first = True
            def qk(di, h=h, hp=hp, base=base):
                pS, pSb = psum()
                MM([(pS[:, i * 128:(i + 1) * 128],
                     KT[base:base + 64, hp, ((gtile0 + i - di) % NKS) * 128:((gtile0 + i - di) % NKS) * 128 + 128],
                     QT[base:base + 64, hp, i * 128:(i + 1) * 128], True, True) for i in range(4)], [KTB, QTB], [pSb])
                pi = nxt("pt", 3)
                A(PT[pi][:, :], pS[:, :], AF.Exp, [pSb, cb], [PTB[pi]], bias=AB[:, h, di:di + 1])
                TT(DVE, PT[pi][:, :].rearrange("p (a c) -> p a c", c=128), PT[pi][:, :].rearrange("p (a c) -> p a c", c=128),
                   MULT[:, di:di + 1, :].to_broadcast([128, 4, 128]), ALU.mult, [PTB[pi], cb], [PTB[pi]])
                return pi
            pis = {0: qk(0), 1: qk(1)}
            for di in range(17):
                if di + 2 < 17:
                    pis[di + 2] = qk(di + 2)
                pi = pis.pop(di)
                items = []
                for i in range(4):
                    slot = (gtile0 + i - di) % NKS
                    items.append((po[:, i * 65:(i + 1) * 65], PT[pi][:, i * 128:(i + 1) * 128], VP[:, slot, h, :], first, False))
                    first = False
                MM(items, [PTB[pi], VPB], [pob])
            for i in range(4):
                CP(ACT, AO[:, i, h, :], po[:, i * 65:(i + 1) * 65], [pob], [AOB[i]])
        for i in range(4):
            attn_finish(128, i, i * 128)

    def hgrn(T, C, own, sample=False, t0=0, seq0=0):
        nch = T // C
        mid = C // 2 - 1
        wi_ = proj_tok_weights(2560)
        for c in range(nch):
            p, pb = proj_tok(wi_, (t0 + c * C, t0 + (c + 1) * C), C)
            CP(ACT, VH[0:C, c, :], p[0:C, :], [pb], [VHB[c]])

        def prep(h):
            KT_, KTB_ = KTILH[h % 2], KTILHB[h % 2]
            QT_, QTB_ = KTILH[2 + h % 2], KTILHB[2 + h % 2]
            sc, scB = SCO[h % 2], SCOB[h % 2]
            pf, pfb = proj_feat(16 + h, T, t0)
            hb = [nxt("ht", 8) for _ in range(5)]
            sg, lf, kk, bcs, dd = [HT[i] for i in hb]
            sgB, lfB, kkB, bcsB, ddB = [HTB[i] for i in hb]
            A(sg[:, 0:T], pf[:, 0:T], AF.Sigmoid, [pfb], [sgB])
            A(lf[:, 0:T], sg[:, 0:T], AF.Ln, [sgB, cb], [lfB], bias=LB[:, h:h + 1], scale=OML[:, h:h + 1])
            TS(DVE, kk[:, 0:T], sg[:, 0:T], NOML[:, h:h + 1], OML[:, h:h + 1], ALU.mult, ALU.add, [sgB, cb], [kkB])
            pq, pqb = proj_feat(12 + h, T, t0)
            A(sg[:, 0:T], pq[:, 0:T], AF.Silu, [pqb], [sgB])
            for c in range(nch):
                K.op(DVE, lambda e, c=c, bcs=bcs, lf=lf: e.tensor_tensor_scan(bcs[:, c * C:(c + 1) * C], ONES[:, 0:C], lf[:, c * C:(c + 1) * C], 0.0, ALU.mult, ALU.add),
                     [lfB, cb], [bcsB])
            b3 = bcs[:, 0:T].rearrange("p (n c) -> p n c", c=C)
            d3 = dd[:, 0:T].rearrange("p (n c) -> p n c", c=C)
            TT(DVE, d3, b3, b3[:, :, mid:mid + 1].to_broadcast([128, nch, C]), ALU.subtract, [bcsB], [ddB])
            A(sc[:, 0:nch], b3[:, :, mid], AF.Exp, [bcsB], [scB])
            A(sc[:, 8:8 + nch], b3[:, :, C - 1], AF.Exp, [bcsB], [scB])
            A(sc[:, 16:16 + nch], d3[:, :, C - 1], AF.Exp, [ddB], [scB])
            A(bcs[:, 0:T], dd[:, 0:T], AF.Exp, [ddB], [bcsB])
            A(lf[:, 0:T], dd[:, 0:T], AF.Exp, [ddB], [lfB], scale=-1.0)
            TT(DVE, KT_[:, 0:T], kk[:, 0:T], lf[:, 0:T], ALU.mult, [kkB, lfB], [KTB_])
            TT(DVE, QT_[:, 0:T], sg[:, 0:T], bcs[:, 0:T], ALU.mult, [sgB, bcsB], [QTB_])
            return (KT_, KTB_, QT_, QTB_, sc, scB)

        def chunks(h, P):
            KT_, KTB_, QT_, QTB_, sc, scB = P
            pO, pOb = psum_acc()
            st1, st2 = {}, {}

            def S1(c):
                if sample:
                    sti = nxt("sst", 3)
                    S, SB_ = SST[sti], SSTB[sti]
                    LOAD(S[:], st_in[seq0 + c, h], SB_)
                else:
                    S, SB_ = STATE[h], STATEB[h]
                pt, ptb = psumh()
                TR([(pt[0:C, 0:128], KT_[:, c * C:(c + 1) * C])], [KTB_], [ptb])
                ki = nxt("ktok", 3)
                CP(ACT, KTOK[ki][0:C, :], pt[0:C, 0:128], [ptb], [KTOKB[ki]])
                st1[c] = (S, SB_, ki)

            def S2(c):
                S, SB_, ki = st1[c]
                pS, pSb = psum()
                MM([(pS[:, 0:128], KTOK[ki][0:C, :], VH[0:C, c, h * 128:(h + 1) * 128], True, True)], [KTOKB[ki], VHB[c]], [pSb])
                pA, pAb = psum()
                MM([(pA[0:C, 0:C], KT_[:, c * C:(c + 1) * C], QT_[:, c * C:(c + 1) * C], True, True)], [KTB_, QTB_], [pAb])
                ai = nxt("atm", 3)
                TT(DVE, ATM[ai][0:C, 0:C], pA[0:C, 0:C], TRI[0:C, 0:C], ALU.mult, [pAb, cb], [ATMB[ai]])
                st2[c] = (pS, pSb, ai)

            def S3(c):
                S, SB_, ki = st1[c]
                pS, pSb, ai = st2[c]
                bi = nxt("sbf", 3)
                TS(DVE, SBF[bi][:], S[:], sc[:, c:c + 1], None, ALU.mult, None, [SB_, scB], [SBFB[bi]])
                MM([(pO[:, c * C:(c + 1) * C], VH[0:C, c, h * 128:(h + 1) * 128], ATM[ai][0:C, 0:C], True, False),
                    (pO[:, c * C:(c + 1) * C], SBF[bi][:], QT_[:, c * C:(c + 1) * C], False, True)],
                   [VHB[c], ATMB[ai], SBFB[bi], QTB_], [pOb])
                ei = nxt("spe", 3)
                TS(DVE, SPE[ei][:], pS[:, 0:128], sc[:, 16 + c:17 + c], None, ALU.mult, None, [pSb, scB], [SPEB[ei]])
                STT(S[:], S[:], sc[:, 8 + c:9 + c], SPE[ei][:], ALU.mult, ALU.add, [SB_, scB, SPEB[ei]], [SB_])
                if sample:
                    STORE(st_s[seq0 + c, h], S[:], SB_)
            for step in range(nch + 2):
                if step < nch:
                    S1(step)
                if 0 <= step - 1 < nch:
                    S2(step - 1)
                if 0 <= step - 2 < nch:
                    S3(step - 2)
            return pO, pOb

        def post(h, pO, pOb):
            ta, tb, tc = TMPA[0], TMPA[1], TMPA[2]
            taB, tbB, tcB = TMPAB[0], TMPAB[1], TMPAB[2]
            CP(ACT, ta[:, 0:T], pO[:, 0:T], [pOb], [taB])
            TT(DVE, tb[:, 0:T], ta[:, 0:T], ta[:, 0:T], ALU.mult, [taB], [tbB])
            pn, pnb = psum()
            MM([(pn[:, 0:T], ONES[:, :], tb[:, 0:T], True, True)], [cb, tbB], [pnb])
            A(tb[:, 0:T], pn[:, 0:T], AF.Sqrt, [pnb, cb], [tbB], bias=EPST[:, 0:1], scale=1.0 / 128)
            RECIP(tb[:, 0:T], tb[:, 0:T], [tbB], [tbB])
            pg, pgb = proj_feat(24 + h, T, t0)
            A(tc[:, 0:T], pg[:, 0:T], AF.Silu, [pgb], [tcB])
            STT(ta[:, 0:T], ta[:, 0:T], GHO[:, h:h + 1], tb[:, 0:T], ALU.mult, ALU.mult, [taB, cb, tbB], [taB])
            TT(DVE, CATT[:, 4 + h, t0:t0 + T], ta[:, 0:T], tc[:, 0:T], ALU.mult, [taB, tcB], [CATB[4 + h]])

        P = prep(0)
        for h in range(4):
            Pn = prep(h + 1) if h + 1 < 4 else None
            pO, pOb = chunks(h, P)
            post(h, pO, pOb)
            P = Pn

    SCO = [sb(f"SCO{i}", [128, 24]) for i in range(2)]
    SCOB = [Buf(f"sco{i}") for i in range(2)]
    KTILX = [sb(f"KTILX{i}", [128, GT], BF16) for i in range(2)]
    KTILH = [KTIL, QTIL, KTILX[0], KTILX[1]]
    KTILHB = [KTILB, QTILB, Buf("ktilx0"), Buf("ktilx1")]
    SCH = [sb(f"SCH{i}", [128, 16]) for i in range(4)]
    SCHB = [Buf(f"sch{i}") for i in range(4)]

    def hg_P1():
        wi_ = proj_tok_weights(2560)
        for c in range(8):
            p, pb = proj_tok(wi_, (c * 64, (c + 1) * 64), 64)
            CP(ACT, VH[0:64, c, :], p[0:64, :], [pb], [VHB[c]])
        for h in range(4):
            pf, pfb = proj_feat(16 + h, GT)
            A(HT[h][:, :], pf[:, :], AF.Sigmoid, [pfb], [HTB[h]])

    def hg_E(g):
        C, nch, mid = 64, 8, 31
        rec = []
        K.rec = rec
        for h in range(4):
            sg, sgB = HT[h], HTB[h]
            lf, lfB = HT[4], HTB[4]
            kk, kkB = HT[5], HTB[5]
            bcs, bcsB = HT[6], HTB[6]
            A(lf[:, :], sg[:, :], AF.Ln, [sgB, cb], [lfB], bias=LB[:, h:h + 1], scale=OML[:, h:h + 1])
            TS(DVE, kk[:, :], sg[:, :], NOML[:, h:h + 1], OML[:, h:h + 1], ALU.mult, ALU.add, [sgB, cb], [kkB])
            for c in range(nch):
                K.op(DVE, lambda e, c=c, bcs=bcs, lf=lf: e.tensor_tensor_scan(bcs[:, c * C:(c + 1) * C], ONES[:, 0:C], lf[:, c * C:(c + 1) * C], 0.0, ALU.mult, ALU.add),
                     [lfB, cb], [bcsB])
            b3 = bcs[:, :].rearrange("p (n c) -> p n c", c=C)
            d3 = lf[:, :].rearrange("p (n c) -> p n c", c=C)
            A(SCH[h][:, 0:8], b3[:, :, C - 1], AF.Exp, [bcsB], [SCHB[h]])
            K.op(DVE, lambda e, b3=b3, h=h: e.tensor_reduce(LDT[:, h:h + 1], b3[:, :, C - 1], AX.X, ALU.add), [bcsB], [ldtB])
            STT(LDACC[:, h:h + 1], LDT[:, h:h + 1], GFL[:, g:g + 1], LDACC[:, h:h + 1], ALU.mult, ALU.add, [ldtB, cb, ldB], [ldB])
            TT(DVE, d3, b3, b3[:, :, mid:mid + 1].to_broadcast([128, nch, C]), ALU.subtract, [bcsB], [lfB])
            A(SCH[h][:, 8:16], d3[:, :, C - 1], AF.Exp, [lfB], [SCHB[h]])
            A(bcs[:, :], lf[:, :], AF.Exp, [lfB], [bcsB], scale=-1.0)
            TT(DVE, KTILH[h][:, :], kk[:, :], bcs[:, :], ALU.mult, [kkB, bcsB], [KTILHB[h]])
        K.rec = None
        return rec

    P2BUF = PT + QB16
    P2BUFB = PTB + QB16B

    def hg_P2_start():
        return {i: p2_trb(*p2_order[i]) for i in range(2)}

    def p2_trb(h, half):
        pt, ptb = psumh()
        TR([(pt[0:64, cc * 128:(cc + 1) * 128], KTILH[h][:, (4 * half + cc) * 64:(4 * half + cc + 1) * 64]) for cc in range(4)], [KTILHB[h]], [ptb])
        bi = nxt("p2b", 5)
        CP(ACT, P2BUF[bi][0:64, :], pt[0:64, 0:512], [ptb], [P2BUFB[bi]])
        return bi
    p2_order = [(h, half) for h in range(4) for half in range(2)]

    def hg_P2(bis=None):
        def trb(h, half):
            pt, ptb = psumh()
            TR([(pt[0:64, cc * 128:(cc + 1) * 128], KTILH[h][:, (4 * half + cc) * 64:(4 * half + cc + 1) * 64]) for cc in range(4)], [KTILHB[h]], [ptb])
            bi = nxt("p2b", 5)
            CP(DVE, P2BUF[bi][0:64, :], pt[0:64, 0:512], [ptb], [P2BUFB[bi]])
            return bi
        order = [(h, half) for h in range(4) for half in range(2)]
        LA = 2
        if bis is None:
            bis = {i: trb(*order[i]) for i in range(LA)}
        for idx, (h, half) in enumerate(order):
            if idx + LA < len(order):
                bis[idx + LA] = trb(*order[idx + LA])
            bi = bis.pop(idx)
            if True:
                pS, pSb = psum_acc()
                MM([(pS[:, cc * 128:(cc + 1) * 128], P2BUF[bi][0:64, cc * 128:(cc + 1) * 128], VH[0:64, 4 * half + cc, h * 128:(h + 1) * 128], True, True)
                    for cc in range(4)], [P2BUFB[bi]] + VHB[4 * half:4 * half + 4], [pSb])
                for cc in range(4):
                    c = 4 * half + cc
                    ei = nxt("spe", 3)
                    A(SPE[ei][:], pS[:, cc * 128:(cc + 1) * 128], AF.Copy, [pSb, SCHB[h]], [SPEB[ei]], scale=SCH[h][:, 8 + c:9 + c])
                    STT(STATE[h][:], STATE[h][:], SCH[h][:, c:c + 1], SPE[ei][:], ALU.mult, ALU.add, [STATEB[h], SCHB[h], SPEB[ei]], [STATEB[h]])

    def outproj(hi, T):
        nt = max(T // 128, 1)
        P = min(T, 128)
        for hf in range(2):
            ws = [wload(S_OUT[ch, :, hf * 512:(hf + 1) * 512], 512, scr="out") for ch in range(8)]
            for tt in range(nt):
                p, pb = psum()
                MM([(p[0:P, :], CATT[:, ch, tt * 128:tt * 128 + P], ws[ch][0][:, 0:512], ch == 0, ch == 7) for ch in range(8)],
                   [w[1] for w in ws] + CATB, [pb])
                TT(DVE, H[hi][0:P, tt, hf * 512:(hf + 1) * 512], p[0:P, :], H[hi][0:P, tt, hf * 512:(hf + 1) * 512], ALU.add, [pb, HB[hi]], [HB[hi]])

    def attn_sample():
        K.op(DVE, lambda e: e.memset(VP[:, 0:7, :, 64:65], 1.0), [], [VPB])
        K.op(DVE, lambda e: e.memset(VP[:, 8:15, :, 64:65], 1.0), [], [VPB])
        KTs = [Buf("kts0"), Buf("kts1")]
        VPs = [Buf("vps0"), Buf("vps1")]
        for x_ in KTs:
            x_.w = list(KTB.w)
            x_.r = dict(KTB.r)
        for x_ in VPs:
            x_.w = list(VPB.w)
            x_.r = dict(VPB.r)
        for b in range(16):
            s0 = 8 * (b % 2)
            for tile in range(7):
                for (src, isk) in ((ck, True), (cv, False)):
                    ci = nxt("ht", 8)
                    if tile < 4:
                        LOAD(CST[ci][:], src[b, 1536 + 128 * tile:1536 + 128 * tile + 128, :], CSTB[ci])
                    else:
                        u = tile - 4
                        v = src[b].rearrange("(m s) c -> m s c", s=16)
                        K.dma_multi(SP, [(CST[ci][32 * t_:32 * t_ + 32, :], v[32 * u:32 * u + 32, t_, :]) for t_ in range(4)],
                                    reads=[], writes=[CSTB[ci]], sb=CSTB[ci])
                    if isk:
                        bi = nxt("qb16", 2)
                        CP(DVE, CB16[bi][:], CST[ci][:], [CSTB[ci]], [CB16B[bi]])
                        pt, ptb = psumh()
                        TR([(pt[:, hp * 128:(hp + 1) * 128], CB16[bi][:, hp * 128:(hp + 1) * 128]) for hp in range(4)], [CB16B[bi]], [ptb])
                        CP(ACT, KT[:, :, (s0 + tile) * 128:(s0 + tile + 1) * 128], pt[:, 0:512].rearrange("p (k c) -> p k c", c=128), [ptb], [KTs[b % 2]])
                    else:
                        CP(ACT, VP[:, s0 + tile, :, 0:64], CST[ci][:].rearrange("p (h e) -> p h e", e=64), [CSTB[ci]], [VPs[b % 2]])
            pS, pSb = psum()
            items = []
            for h in range(8):
                hp, base = h // 2, 64 * (h % 2)
                for tile in range(7):
                    items.append((pS[:, (h * 7 + tile) * 4:(h * 7 + tile) * 4 + 4], KT[base:base + 64, hp, (s0 + tile) * 128:(s0 + tile + 1) * 128],
                                  QT[base:base + 64, hp, 4 * b:4 * b + 4], True, True))
                items.append((pS[0:64, 224 + h * 4:224 + h * 4 + 4], KT[base:base + 64, hp, 7 * 128:7 * 128 + 64],
                              QT[base:base + 64, hp, 4 * b:4 * b + 4], True, True))
            MM(items, [KTB, KTs[b % 2], QTB], [pSb])
            pi = nxt("pt", 3)
            ti = nxt("tmpa", 3)
            A(TMPA[ti][:, 0:224], pS[:, 0:224], AF.Exp, [pSb], [TMPAB[ti]])
            A(TMPA[ti][0:64, 224:256], pS[0:64, 224:256], AF.Exp, [pSb], [TMPAB[ti]])
            TT(DVE, PT[pi][:, 0:224], TMPA[ti][:, 0:224], WS[:, :], ALU.mult, [TMPAB[ti], cb], [PTB[pi]])
            TT(DVE, PT[pi][0:64, 224:256], TMPA[ti][0:64, 224:256], WN[:, b, :], ALU.mult, [TMPAB[ti], cb], [PTB[pi]])
            pos = [psum_acc(), psum_acc()]
            for half in range(2):
                items = []
                first = True
                for hh in range(4):
                    h = half * 4 + hh
                    for tile in range(7):
                        items.append((pos[half][0][0:4, hh * 65:(hh + 1) * 65], PT[pi][:, (h * 7 + tile) * 4:(h * 7 + tile) * 4 + 4],
                                      VP[:, s0 + tile, h, :], first, False))
                        first = False
                    items.append((pos[half][0][0:4, hh * 65:(hh + 1) * 65], PT[pi][0:64, 224 + h * 4:224 + h * 4 + 4], VP[0:64, 7, h, :], False, False))
                MM(items, [PTB[pi], VPB, VPs[b % 2]], [pos[half][1]])
            qi = 1 + b % 3
            for half in range(2):
                CP(ACT, AO[0:4, qi, 4 * half:4 * half + 4, :], pos[half][0][0:4, 0:260].rearrange("p (h e) -> p h e", e=65), [pos[half][1]], [AOB[qi]])
            STORE(S_AO[4 * b:4 * b + 4, :], AO[0:4, qi, :, :].rearrange("p h e -> p (h e)"), AOB[qi], writes=[aoB])
        aoB.w = [(b_.dsem, 16 * b_.dcount) for b_ in AOB[1:4] if b_.dsem is not None]
        LOAD(AO[0:64, 0, :, :].rearrange("p h e -> p (h e)"), S_AO, AOB[0], reads=[aoB])
        attn_finish(64, 0, 0)

    def group(hi, T, kind, gidx, src_rows, out_rows, nextload, skip_norm1=False):
        if not skip_norm1:
            norm_T(hi, T, 0)
        ffn(hi, T, 0, prefetch=nextload)
        norm_T(hi, T, 1)
        if kind == 3:
            attn_kv(T, 7, kout=k_s, vout=v_s, kvrows=0)
            attn_q(T)
            attn_sample()
            hgrn(32, 4, True, sample=True, t0=0, seq0=0)
            hgrn(32, 4, True, sample=True, t0=32, seq0=8)
        else:
            if kind >= 1:
                attn_kv(T, gidx * 4, kout=k_p if kind == 2 else None, vout=v_p if kind == 2 else None,
                        kvrows=(gidx - OWN0) * GT if kind == 2 else None, halo=(kind == 1))
            if kind == 2:
                attn_q(T)
                attn_prompt(gidx * 4)
            hgrn(T, 64, kind == 2)
        if kind >= 2:
            outproj(hi, T)
            norm_T(hi, T, 2)
            ffn(hi, T, 1)
            if kind == 3:
                STORE(out_rows, H[hi][0:T, 0, :], HB[hi])
            else:
                STORE(out_rows.rearrange("(t p) d -> p t d", p=128), H[hi][:, :, :], HB[hi])

    for h in range(4):
        K.op(DVE, lambda e, h=h: e.memset(STATE[h][:], 0.0), [], [STATEB[h]])
    load_x(1, xp[0:GT, :], GT)

    parc = {}

    def get_par(e):
        if "p" not in parc:
            parc["p"] = e.partition_id() % 2
        return parc["p"]

    def sh_row(e, mine, seg):
        par = get_par(e)
        who = par if mine else ((par + 1) % 2)
        return SH_S2[bass.ds(who * 256 + seg * 128, 128), :]

    def sh_l(e, mine):
        par = get_par(e)
        who = par if mine else ((par + 1) % 2)
        return SH_L2[bass.ds(who * 128, 128), :]

    def seg_boundary():
        K.dma(POOL, (lambda e: sh_row(e, True, 0)), ST4[:].rearrange("p h c -> p (h c)"), reads=STATEB, writes=[], sb=STATEB[0])
        CP(DVE, LDM[:, 0:4], LDACC[:, :], [ldB], [ldmB])
        K.op(DVE, lambda e: e.memset(LDACC[:], 0.0), [], [ldB])
        for h in range(4):
            K.op(DVE, lambda e, h=h: e.memset(STATE[h][:], 0.0), [], [STATEB[h]])

    def exchange():
        CP(DVE, LDM[:, 4:8], LDACC[:, :], [ldB], [ldmB])
        K.dma(POOL, (lambda e: sh_row(e, True, 1)), ST4[:].rearrange("p h c -> p (h c)"), reads=STATEB, writes=[], sb=STATEB[0])
        K.dma(POOL, (lambda e: sh_l(e, True)), LDM[:, :], reads=[ldmB], writes=[], sb=ldmB)
        pubB = Buf("pub")
        pubB.w = [(b.dsem, 16 * b.dcount) for b in [STATEB[0], ldmB]]
        flgB = Buf("flg")
        K.dma(POOL, (lambda e: SH_F[bass.ds(get_par(e), 1), :]), tok, reads=[pubB], writes=[flgB], sb=flgB)

        K.dma(POOL, HT[4][:], (lambda e: sh_row(e, True, 0)), reads=[], writes=[HTB[4]], sb=HTB[4])

        def cond(e):
            kw = dict(allow_slow_non_contiguous=True)
            with e.register("rx") as rx, e.register("rf0") as rf0, e.register("rf1") as rf1, e.register("rt") as rt, e.register("rd") as rd:
                e.load(rx, needx[0:1, 0:1])
                with e.If_ne(rx, 0):
                    e.load(rt, tok[0:1, 0:1])
                    e.reg_mov(rd, 1)
                    with e.While(rd):
                        e.load(rf0, SH_F[0:1, 0:1])
                        e.load(rf1, SH_F[1:2, 0:1])
                        e.reg_sub(rf0, rf0, rt)
                        e.reg_sub(rf1, rf1, rt)
                        e.reg_alu(rd, rf0, rf1, ALU.bitwise_or)
                    e.dma_start(out=LDP[:, :], in_=sh_l(e, False), **kw).then_inc(ldpB.dsem, 16)
                    e.dma_start(out=HT[5][:], in_=sh_row(e, False, 0), **kw).then_inc(HTB[5].dsem, 16)
                    e.dma_start(out=HT[6][:], in_=sh_row(e, False, 1), **kw).then_inc(HTB[6].dsem, 16)
                with e.Else():
                    e.dma_start(out=LDP[:, :], in_=zer[:, 0:8], **kw).then_inc(ldpB.dsem, 16)
                    e.dma_start(out=HT[5][:], in_=zer[:, :], **kw).then_inc(HTB[5].dsem, 16)
                    e.dma_start(out=HT[6][:], in_=zer[:, :], **kw).then_inc(HTB[6].dsem, 16)
        K.raw(POOL, cond, reads=[flgB], writes=[ldpB, HTB[5], HTB[6]], dma_bufs=[ldpB, HTB[5], HTB[6]])
        A(LDM[:, :], LDM[:, :], AF.Exp, [ldmB], [ldmB])
        A(LDP[:, :], LDP[:, :], AF.Exp, [ldpB], [ldpB])
        for h in range(4):
            S1m, S1p, S2p = HT[4][:, h * 128:(h + 1) * 128], HT[5][:, h * 128:(h + 1) * 128], HT[6][:, h * 128:(h + 1) * 128]
            D1m, D2m = LDM[:, h:h + 1], LDM[:, 4 + h:5 + h]
            D1p, D2p = LDP[:, h:h + 1], LDP[:, 4 + h:5 + h]
            STT(SPE[0][:], S1p, D1m, S1m, ALU.mult, ALU.add, [HTB[4], HTB[5], ldmB], [SPEB[0]])
            STT(SPE[1][:], S1m, D1p, S1p, ALU.mult, ALU.add, [HTB[4], HTB[5], ldpB], [SPEB[1]])
            STT(SPE[1][:], SPE[1][:], D2p, S2p, ALU.mult, ALU.add, [SPEB[1], HTB[6], ldpB], [SPEB[1]])
            TS(DVE, SPE[0][:], SPE[0][:], ROLE[:, 0:1], None, ALU.mult, None, [SPEB[0], cb], [SPEB[0]])
            STT(SPE[1][:], SPE[1][:], ROLE[:, 1:2], SPE[0][:], ALU.mult, ALU.add, [SPEB[1], SPEB[0], cb], [SPEB[1]])
            STT(STATE[h][:], SPE[1][:], D2m, STATE[h][:], ALU.mult, ALU.add, [SPEB[1], ldmB, STATEB[h]], [STATEB[h]])

    pendE, pendP2 = [], False
    norm_T(1, GT, 0)
    for g in range(NG):
        hi = (g + 1) % 2
        kind = 2 if g >= OWN0 else (1 if g >= HALO0 else 0)
        nl = (lambda g=g, hi=hi: load_x(1 - hi, xp[(g + 1) * GT:(g + 2) * GT, :], GT)) if g + 1 < NG else (lambda: load_x(1, xs, 64))
        orow = y_p[(g - OWN0) * GT:(g - OWN0 + 1) * GT, :] if kind == 2 else None
        if kind == 2:
            while pendE:
                pendE.pop(0)()
            if pendP2:
                hg_P2()
                pendP2 = False
            if g == OWN0:
                pc_need("dn1")
                exchange()
            group(hi, GT, kind, g, None, orow, nl, skip_norm1=(g == OWN0))
            continue
        ffn_gateup(hi, GT, 0, drain=pendE)
        pc_need("dn0")
        ffn_down(hi, GT, 0, prefetch=nl)
        p2s = hg_P2_start() if pendP2 else None
        sis1 = norm_stats(hi, GT)
        if pendP2:
            hg_P2(p2s)
        if g == HALO0:
            seg_boundary()
        pc_need("in")
        norm_apply(hi, GT, 1, sis1)
        if kind == 1:
            if g == HALO0:
                KTB.r.update(stg_final())
                VPB.r.update(stg_final())
            attn_kv(GT, g * 4, halo=True)
        sis = norm_stats(1 - hi, GT)
        hg_P1()
        pendE = hg_E(g)
        pendP2 = True
        norm_apply(1 - hi, GT, 0, sis)
    for h in range(4):
        STORE(st_p[h], STATE[h][:], STATEB[h])
    group(1, 64, 3, 0, xs, y_s, None)

    final_waits = [(b.dsem, 16 * b.dcount) for b in K.dbufs]
    engs = {"tensor": PE, "scalar": ACT, "vector": DVE, "gpsimd": POOL, "sync": SP}
    with nc.Block() as block:
        def emit(e, eng, final=False):
            for (wl, fn, sem, inc) in eng.prog:
                for (s_, v_) in wl:
                    e.wait_ge(s_, v_)
                if sem is None:
                    fn(e)
                else:
                    fn(e).then_inc(sem, inc)
            if final:
                for (s_, v_) in final_waits:
                    e.wait_ge(s_, v_)

        @block.tensor
        def _(e):
            emit(e, PE)

        @block.scalar
        def _(e):
            emit(e, ACT)

        @block.vector
        def _(e):
            emit(e, DVE)

        @block.gpsimd
        def _(e):
            emit(e, POOL, final=True)

        @block.sync
        def _(e):
            emit(e, SP)
    es.close()
    return nc


def _consts(core):
    slopes = np.array([2.0 ** (-8.0 * (h + 1) / 8) for h in range(8)], np.float64)

    def mult(dist):
        dist = np.asarray(dist)
        m = ((dist >= 0) & (dist <= 128)).astype(np.float64)
        m += ((dist >= 0) & (dist % 4 == 0) & (dist <= 512))
        m += ((dist >= 0) & (dist % 16 == 0) & (dist <= 2048))
        return m
    ki = np.arange(128)[:, None]
    qi = np.arange(128)[None, :]
    c_mult = np.zeros((128, 17, 128), np.float32)
    for di in range(17):
        c_mult[:, di, :] = mult(128 * di + qi - ki)
    c_ab = np.zeros((128, 8, 17), np.float32)
    for h in range(8):
        for di in range(17):
            c_ab[:, h, di] = -slopes[h] * (128 * di + 64 - np.arange(128))
    rows = np.zeros((7, 128), np.int64)
    for t in range(4):
        rows[t] = 1536 + 128 * t + np.arange(128)
    for u in range(3):
        p = np.arange(128)
        rows[4 + u] = 16 * (32 * u + p % 32) + p // 32
    c_ws = np.zeros((128, 8, 7, 4), np.float64)
    for h in range(8):
        for tile in range(7):
            for t in range(4):
                dist = 2048 + t - rows[tile]
                c_ws[:, h, tile, t] = mult(dist) * np.exp(-slopes[h] * dist)
    c_wn = np.zeros((64, 16, 8, 4), np.float64)
    for j in range(64):
        b_, t_ = j // 4, j % 4
        for t in range(4):
            if t_ <= t:
                dist = t - t_
                for h in range(8):
                    c_wn[j, b_, h, t] = mult(dist) * np.exp(-slopes[h] * dist)
    c_tri = (np.arange(64)[:, None] <= np.arange(64)[None, :]).astype(np.float32)
    return dict(
        c_mult=c_mult.reshape(128, -1), c_ab=c_ab.reshape(128, -1),
        c_ws=c_ws.reshape(128, -1).astype(np.float32), c_wn=c_wn.reshape(64, -1).astype(np.float32),
        c_tri=c_tri, c_id=np.eye(128, dtype=np.float32),
        c_flag=np.full((128, 1), 0.0 if core == 0 else 1.0, np.float32),
    )


_NC = None


def kernel(x_prompt, x_sample, cache_k, cache_v, state_hgrn, norm_ffn1, ffn1_w_gate_up, ffn1_w_down, norm_mix, w_in,
           q_norm, k_norm, gamma_lb, attn_out_norm, hgrn_out_norm, w_out, norm_ffn2, ffn2_w_gate_up, ffn2_w_down):
    global _NC
    if _NC is None:
        _NC = build_program()
    nc = _NC
    f = lambda a: np.ascontiguousarray(np.asarray(a, dtype=np.float32))
    xpf = f(x_prompt)[0]
    in_maps = []
    token = np.full((1, 16), int(np.random.randint(1, 2 ** 31 - 1)), np.int32)
    for c in range(NCORE):
        k, r = c // 2, c % 2
        n = max(2 * k - 1, 0)
        nB = (n + 1) // 2
        seg1 = list(range(0, nB)) if r == 1 else list(range(nB, n))
        seg2 = (2 * k) if r == 1 else ((2 * k - 1) if k >= 1 else None)
        xp_c = np.zeros((NSL * SLICE, D), np.float32)
        gfl = np.zeros((128, 16), np.float32)
        for i, sl in enumerate(seg1):
            pos = 3 - len(seg1) + i
            xp_c[pos * SLICE:(pos + 1) * SLICE] = xpf[sl * SLICE:(sl + 1) * SLICE]
            gfl[:, pos * 4:(pos + 1) * 4] = 1.0
        if seg2 is not None:
            xp_c[3 * SLICE:4 * SLICE] = xpf[seg2 * SLICE:(seg2 + 1) * SLICE]
            gfl[:, 12:16] = 1.0
        xp_c[4 * SLICE:5 * SLICE] = xpf[c * SLICE:(c + 1) * SLICE]
        role = np.zeros((128, 2), np.float32)
        role[:, 0] = 1.0 if r == 0 else 0.0
        role[:, 1] = 1.0 - role[:, 0]
        m = dict(
            xp=xp_c, xs=f(x_sample)[16 * c:16 * c + 16].reshape(64, D),
            ck=f(cache_k)[0, 16 * c:16 * c + 16].reshape(16, 2048, 512),
            cv=f(cache_v)[0, 16 * c:16 * c + 16].reshape(16, 2048, 512),
            st=f(state_hgrn)[0, 16 * c:16 * c + 16],
            w_gu1=f(ffn1_w_gate_up)[0], w_d1=f(ffn1_w_down)[0], w_gu2=f(ffn2_w_gate_up)[0], w_d2=f(ffn2_w_down)[0],
            w_in=f(w_in)[0], w_out=f(w_out)[0], g1=f(norm_ffn1), gm=f(norm_mix), g2=f(norm_ffn2),
            gq=f(q_norm), gk=f(k_norm), glb=f(gamma_lb), gao=f(attn_out_norm), gho=f(hgrn_out_norm),
        )
        m.update(_consts(c))
        m.update(c_gfl=gfl, c_role=role, tok=token, needx=np.full((1, 16), 1 if c >= 2 else 0, np.int32),
                 zer=np.zeros((128, 512), np.float32))
        in_maps.append(m)
    res = run_bass_kernel_spmd(nc, in_maps, core_ids=list(range(NCORE)))
    R = res.results
    y_prompt = np.concatenate([R[c]["y_p"] for c in range(NCORE)], 0)[None]
    y_sample = np.concatenate([R[c]["y_s"].reshape(16, 4, D) for c in range(NCORE)], 0)
    nkp = R[7]["k_p"].reshape(1, 1, 2048, 8, 64)
    nvp = R[7]["v_p"].reshape(1, 1, 2048, 8, 64)
    nsp = R[7]["st_p"].reshape(1, 1, 4, 128, 128)
    nks = np.concatenate([R[c]["k_s"].reshape(16, 4, 8, 64) for c in range(NCORE)], 0)[None]
    nvs = np.concatenate([R[c]["v_s"].reshape(16, 4, 8, 64) for c in range(NCORE)], 0)[None]
    nss = np.concatenate([R[c]["st_s"] for c in range(NCORE)], 0)[None]
    return (y_prompt.astype(np.float32), y_sample.astype(np.float32), nkp.astype(np.float32), nvp.astype(np.float32),
            nsp.astype(np.float32), nks.astype(np.float32), nvs.astype(np.float32), nss.astype(np.float32))
```

```python
import os
from contextlib import ExitStack
import numpy as np
import concourse.bass as bass
import concourse.mybir as mybir
from concourse.bass_utils import run_bass_kernel_spmd

F32 = mybir.dt.float32
BF16 = mybir.dt.bfloat16
I32 = mybir.dt.int32
AF = mybir.ActivationFunctionType
ALU = mybir.AluOpType
AX = mybir.AxisListType

D = 1024
DFF = 2816
NCORE = 8
SLICE = 2048
NSL = 5
GT = 512
NG = NSL * SLICE // GT
OWN0 = NG - 4
HALO0 = NG - 8
EPS = 1e-6
NKS = 20
RING = 10


class Eng:
    def __init__(self, name, sem):
        self.name = name
        self.sem = sem
        self.count = 0
        self.prog = []
        self.seen = {}


class Buf:
    def __init__(self, name=""):
        self.name = name
        self.w = []
        self.r = {}
        self.dsem = None
        self.dcount = 0


class Builder:
    def __init__(self, nc, es):
        self.nc = nc
        self.es = es
        self.nsem = 0
        self.dbufs = []
        self.rec = None

    def sem(self, name):
        self.nsem += 1
        return self.es.enter_context(self.nc.semaphore(f"{name}_{self.nsem}"))

    def _waits(self, eng, reads, writes):
        waits = {}

        def need(sv):
            sem, val = sv
            if waits.get(id(sem), (None, 0))[1] < val:
                waits[id(sem)] = (sem, val)
        for b in reads:
            for sv in b.w:
                need(sv)
        for b in writes:
            for sv in b.w:
                need(sv)
            for sv in b.r.values():
                need(sv)
        wl = []
        for sem, val in waits.values():
            if eng.seen.get(id(sem), 0) >= val:
                continue
            eng.seen[id(sem)] = val
            wl.append((sem, val))
        return wl

    def op(self, eng, fn, reads=(), writes=()):
        if self.rec is not None:
            rec, self_ = self.rec, self
            reads, writes = list(reads), list(writes)

            def thunk():
                saved, self_.rec = self_.rec, None
                self_.op(eng, fn, reads, writes)
                self_.rec = saved
            rec.append(thunk)
            return
        wl = self._waits(eng, reads, writes)
        eng.count += 1
        me = (eng.sem, eng.count)
        eng.prog.append((wl, fn, eng.sem, 1))
        for b in writes:
            b.w = [me]
            b.r = {}
        for b in reads:
            b.r[id(eng.sem)] = me

    def raw(self, eng, fn, reads=(), writes=(), dma_bufs=()):
        wl = self._waits(eng, reads, writes)
        for b in dma_bufs:
            if b.dsem is None:
                b.dsem = self.sem("d")
                self.dbufs.append(b)
            b.dcount += 1
            b.w = [(b.dsem, 16 * b.dcount)]
            b.r = {}
        eng.prog.append((wl, fn, None, 0))

    def dma_multi(self, q, pairs, reads=(), writes=(), sb=None):
        wl = self._waits(q, reads, writes)
        if sb.dsem is None:
            sb.dsem = self.sem("d")
            self.dbufs.append(sb)
        for i, (out, in_) in enumerate(pairs):
            def fn(e, out=out, in_=in_):
                return e.dma_start(out=out, in_=in_, allow_slow_non_contiguous=True)
            q.prog.append((wl if i == 0 else [], fn, sb.dsem, 16))
        sb.dcount += len(pairs)
        me = (sb.dsem, 16 * sb.dcount)
        for b in writes:
            b.w = [me]
            b.r = {}
        for b in reads:
            b.r[id(sb.dsem)] = me

    def dma(self, q, out, in_, reads=(), writes=(), sb=None):
        wl = self._waits(q, reads, writes)
        if sb.dsem is None:
            sb.dsem = self.sem("d")
            self.dbufs.append(sb)
        sb.dcount += 1
        me = (sb.dsem, 16 * sb.dcount)

        def fn(e, out=out, in_=in_):
            o = out(e) if callable(out) else out
            i = in_(e) if callable(in_) else in_
            return e.dma_start(out=o, in_=i, allow_slow_non_contiguous=True)
        q.prog.append((wl, fn, sb.dsem, 16))
        for b in writes:
            b.w = [me]
            b.r = {}
        for b in reads:
            b.r[id(sb.dsem)] = me


def build_program():
    nc = bass.Bass("TRN2", target_bir_lowering=False)
    es = ExitStack()
    K = Builder(nc, es)

    def din(name, shape, dt=F32):
        return nc.dram_tensor(name, list(shape), dt, kind="ExternalInput").ap()

    def dout(name, shape):
        return nc.dram_tensor(name, list(shape), F32, kind="ExternalOutput").ap()

    def dtmp(name, shape, dt):
        return nc.dram_tensor(name, list(shape), dt, kind="Internal").ap()

    def sb(name, shape, dt=F32):
        return es.enter_context(nc.sbuf_tensor(name, list(shape), dt))

    def ps(name, shape, dt=F32):
        return es.enter_context(nc.psum_tensor(name, list(shape), dt))

    xp = din("xp", [NSL * SLICE, D])
    xs = din("xs", [64, D])
    ck = din("ck", [16, 2048, 512])
    cv = din("cv", [16, 2048, 512])
    st_in = din("st", [16, 4, 128, 128])
    w_gu = [din("w_gu1", [D, 2 * DFF]), din("w_gu2", [D, 2 * DFF])]
    w_dn = [din("w_d1", [DFF, D]), din("w_d2", [DFF, D])]
    w_in = din("w_in", [D, 3584])
    w_out = din("w_out", [D, D])
    g_n = [din("g1", [1, D]), din("gm", [1, D]), din("g2", [1, D])]
    g_q = din("gq", [1, 64])
    g_k = din("gk", [1, 64])
    g_lb = din("glb", [2, 512])
    g_ao = din("gao", [1, 512])
    g_ho = din("gho", [1, 512])
    c_mult = din("c_mult", [128, 17 * 128])
    c_ab = din("c_ab", [128, 8 * 17])
    c_ws = din("c_ws", [128, 224])
    c_wn = din("c_wn", [64, 16 * 32])
    c_tri = din("c_tri", [64, 64])
    c_id = din("c_id", [128, 128])
    c_flag = din("c_flag", [128, 1])
    c_gfl = din("c_gfl", [128, 16])
    c_role = din("c_role", [128, 2])
    tok = din("tok", [1, 16], I32)
    needx = din("needx", [1, 16], I32)
    zer = din("zer", [128, 512])
    SH_S2 = nc.dram_tensor("sh_s", [2 * 2 * 128, 512], F32, kind="Internal", addr_space="Shared").ap()
    SH_L2 = nc.dram_tensor("sh_l", [2 * 128, 8], F32, kind="Internal", addr_space="Shared").ap()
    SH_F = nc.dram_tensor("sh_f", [2, 16], I32, kind="Internal", addr_space="Shared").ap()

    y_p = dout("y_p", [SLICE, D])
    y_s = dout("y_s", [64, D])
    k_p = dout("k_p", [SLICE, 512])
    v_p = dout("v_p", [SLICE, 512])
    st_p = dout("st_p", [4, 128, 128])
    k_s = dout("k_s", [64, 512])
    v_s = dout("v_s", [64, 512])
    st_s = dout("st_s", [16, 4, 128, 128])

    S_GU = [dtmp(f"s_gu{i}", [22, 2, 128, 8, 128], BF16) for i in range(2)]
    S_DN = [dtmp(f"s_dn{i}", [22, 128, D], BF16) for i in range(2)]
    S_INL = dtmp("s_inl", [28, 128, 8, 128], BF16)
    S_INR = dtmp("s_inr", [8, 128, 3584], BF16)
    S_OUT = dtmp("s_out", [8, 128, D], BF16)
    S_AO = dtmp("s_ao", [64, 8 * 65], F32)
    scratchB = Buf("scratch")
    scrB = {}
    aoB = Buf("ao_scratch")

    PE = Eng("pe", K.sem("pe"))
    ACT = Eng("act", K.sem("act"))
    DVE = Eng("dve", K.sem("dve"))
    POOL = Eng("pool", K.sem("pool"))
    SP = Eng("sp", K.sem("sp"))

    H = [sb(f"H{i}", [128, 4, D]) for i in range(2)]
    HB = [Buf(f"H{i}") for i in range(2)]
    XT = sb("XT", [128, 8, GT], BF16)
    XTB = Buf("XT")
    XN = sb("XN", [128, D], BF16)
    XNB = Buf("XN")
    XNB2 = [Buf("XNa"), Buf("XNb")]
    ACTT = sb("ACTT", [128, 22, GT], BF16)
    ACTTB = [Buf(f"actt{j}") for j in range(22)]
    RNG = sb("RNG", [128, RING, 1024], BF16)
    RNGB = [Buf(f"ring{i}") for i in range(RING)]
    TMPA = [sb(f"TMPA{i}", [128, GT]) for i in range(3)]
    TMPAB = [Buf(f"tmpa{i}") for i in range(3)]
    KT = sb("KT", [128, 4, NKS * 128], BF16)
    KTB = Buf("KT")
    VPraw = sb("VP", [128, NKS * 8 * 65], BF16)
    VP = VPraw[:].rearrange("p (a h e) -> p a h e", a=NKS, h=8)
    VPB = Buf("VP")
    QT = sb("QT", [128, 4, GT], BF16)
    QTB = Buf("QT")
    CATT = sb("CATT", [128, 8, GT], BF16)
    CATB = [Buf(f"cat{i}") for i in range(8)]
    MULT = sb("MULT", [128, 17, 128], BF16)
    AB = sb("AB", [128, 8, 17])
    WS = sb("WS", [128, 224])
    WN = sb("WN", [64, 16, 32])
    TRI = sb("TRI", [64, 64])
    IDB = sb("IDB", [128, 128], BF16)
    ONES = sb("ONES", [128, 128])
    FLAG = sb("FLAG", [128, 1])
    GFL = sb("GFL", [128, 16])
    ROLE = sb("ROLE", [128, 2])
    LDACC = sb("LDACC", [128, 4])
    LDT = sb("LDT", [128, 4])
    LDM = sb("LDM", [128, 8])
    LDP = sb("LDP", [128, 8])
    ldB = Buf("ldacc")
    ldtB = Buf("ldt")
    ldmB = Buf("ldm")
    ldpB = Buf("ldp")
    EPST = sb("EPST", [128, 1])
    GT3 = sb("GT3", [128, 3, 8])
    GQ = sb("GQ", [128, 64])
    GK = sb("GK", [128, 64])
    GAOT = sb("GAOT", [128, 4])
    GHO = sb("GHO", [128, 4])
    LBT = sb("LBT", [128, 2, 4])
    LB = sb("LB", [128, 4])
    OML = sb("OML", [128, 4])
    NOML = sb("NOML", [128, 4])
    constB = Buf("const")
    QKV = [sb(f"QKV{i}", [128, 512]) for i in range(3)]
    QKVB = [Buf(f"qkv{i}") for i in range(3)]
    QB16 = [sb(f"QB16_{i}", [128, 512], BF16) for i in range(2)]
    QB16B = [Buf(f"qb16{i}") for i in range(2)]
    SM = [sb(f"SM{i}", [128, 16]) for i in range(6)]
    SMB = [Buf(f"sm{i}") for i in range(6)]
    PT = [sb(f"PT{i}", [128, GT], BF16) for i in range(3)]
    PTB = [Buf(f"pt{i}") for i in range(3)]
    AO = sb("AO", [128, 4, 8, 65])
    AOB = [Buf(f"ao{i}") for i in range(4)]
    VH = sb("VH", [64, 8, 512], BF16)
    VHB = [Buf(f"vh{i}") for i in range(8)]
    HT = [sb(f"HT{i}", [128, GT]) for i in range(8)]
    HTB = [Buf(f"ht{i}") for i in range(8)]
    QTIL = sb("QTIL", [128, GT], BF16)
    QTILB = Buf("qtil")
    KTIL = sb("KTIL", [128, GT], BF16)
    KTILB = Buf("ktil")
    KTOK = [sb(f"KTOK{i}", [64, 128], BF16) for i in range(3)]
    KTOKB = [Buf(f"ktok{i}") for i in range(3)]
    ATM = [sb(f"ATM{i}", [64, 64], BF16) for i in range(3)]
    ATMB = [Buf(f"atm{i}") for i in range(3)]
    SPE = [sb(f"SPE{i}", [128, 128]) for i in range(3)]
    SPEB = [Buf(f"spe{i}") for i in range(3)]
    SBF = [sb(f"SBF{i}", [128, 128], BF16) for i in range(3)]
    SBFB = [Buf(f"sbf{i}") for i in range(3)]
    ST4 = sb("ST4", [128, 4, 128])
    STATE = [ST4[:, i, :] for i in range(4)]
    STATEB = [Buf(f"state{i}") for i in range(4)]
    SST = [sb(f"SST{i}", [128, 128]) for i in range(3)]
    SSTB = [Buf(f"sst{i}") for i in range(3)]
    CST, CSTB = HT, HTB
    CB16, CB16B = QB16, QB16B

    PS = [ps(f"PS{i}", [128, 512]) for i in range(6)]
    PSB = [Buf(f"ps{i}") for i in range(6)]
    PSH = [ps(f"PSH{i}", [128, 1024], BF16) for i in range(2)]
    PSHB = [Buf(f"psh{i}") for i in range(2)]
    rr = {"ps6": 0, "p2b": 0, "psa": 0, "ps": 0, "psh": 0, "ring": 0, "tmpa": 0, "qkv": 0, "qb16": 0, "sm": 0, "pt": 0, "ht": 0,
          "ktok": 0, "atm": 0, "spe": 0, "sbf": 0, "cst": 0, "cb16": 0, "sst": 0}

    def nxt(key, n):
        i = rr[key]
        rr[key] = (i + 1) % n
        return i

    def psum():
        i = nxt("ps", 4)
        return PS[i], PSB[i]

    def psum6():
        i = nxt("ps6", 6)
        return PS[i], PSB[i]

    def psum_acc():
        i = 4 + nxt("psa", 2)
        return PS[i], PSB[i]

    def psumh():
        i = nxt("psh", 2)
        return PSH[i], PSHB[i]

    def MM(items, reads, writes):
        def fn(e, items=items):
            last = None
            for (o, l, r, st, sp_) in items:
                last = e.matmul(o, l, r, start=st, stop=sp_, skip_group_check=True)
            return last
        K.op(PE, fn, reads, writes)

    def TR(items, reads, writes):
        def fn(e, items=items):
            last = None
            for (o, i) in items:
                last = e.transpose(o, i, IDB[0:i.shape[0], 0:i.shape[0]])
            return last
        K.op(PE, fn, reads + [constB], writes)

    def A(out, in_, func, reads, writes, bias=None, scale=None, accum_out=None):
        def fn(e):
            kw = {}
            if bias is not None:
                kw["bias"] = bias
            if scale is not None:
                kw["scale"] = scale
            if accum_out is not None:
                kw["accum_out"] = accum_out
            return e.activation(out, in_, func, **kw)
        K.op(ACT, fn, reads, writes)

    def TT(eng, out, in0, in1, op, reads, writes):
        K.op(eng, lambda e: e.tensor_tensor(out, in0, in1, op), reads, writes)

    def TS(eng, out, in0, s1, s2, op0, op1, reads, writes):
        if op1 is None:
            K.op(eng, lambda e: e.tensor_scalar(out, in0, s1, None, op0), reads, writes)
        else:
            K.op(eng, lambda e: e.tensor_scalar(out, in0, s1, s2, op0, op1), reads, writes)

    def STT(out, in0, scalar, in1, op0, op1, reads, writes):
        K.op(DVE, lambda e: e.scalar_tensor_tensor(out, in0, scalar, in1, op0, op1), reads, writes)

    def CP(eng, out, in_, reads, writes):
        if eng is ACT:
            K.op(eng, lambda e: e.copy(out, in_), reads, writes)
        else:
            K.op(eng, lambda e: e.tensor_copy(out, in_), reads, writes)

    def RECIP(out, in_, reads, writes):
        K.op(DVE, lambda e: e.reciprocal(out, in_), reads, writes)

    def LOAD(out, in_, b, reads=()):
        K.dma(SP, out, in_, reads=list(reads), writes=[b], sb=b)

    def STORE(out, in_, b, writes=()):
        K.dma(POOL, out, in_, reads=[b], writes=list(writes), sb=b)

    cb = constB
    LOAD(AB[:].rearrange("p h d -> p (h d)"), c_ab, cb)
    LOAD(WS[:], c_ws, cb)
    LOAD(WN[:].rearrange("p b c -> p (b c)"), c_wn, cb)
    LOAD(TRI[:], c_tri, cb)
    LOAD(FLAG[:], c_flag, cb)
    LOAD(GFL[:], c_gfl, cb)
    LOAD(ROLE[:], c_role, cb)
    K.op(DVE, lambda e: e.memset(LDACC[:], 0.0), [], [ldB])
    for i in range(3):
        LOAD(GT3[:, i, :], g_n[i].rearrange("o (k p) -> p (o k)", p=128), cb)
    LOAD(GAOT[:], g_ao.rearrange("o (k p) -> p (o k)", p=128), cb)
    LOAD(GHO[:], g_ho.rearrange("o (h p) -> p (o h)", p=128), cb)
    LOAD(LBT[:], g_lb.rearrange("r (h p) -> p r h", p=128), cb)
    LOAD(GQ[:, :], g_q[0:1, :].to_broadcast([128, 64]), cb)
    LOAD(GK[:, :], g_k[0:1, :].to_broadcast([128, 64]), cb)
    LOAD(H[0][:, 0, 0:128], c_id, HB[0])
    LOAD(H[0][:, 1:4, :].rearrange("p a d -> p (a d)")[:, 0:2176], c_mult, HB[0])
    CP(DVE, IDB[:], H[0][:, 0, 0:128], [HB[0]], [cb])
    CP(DVE, MULT[:].rearrange("p a b -> p (a b)"), H[0][:, 1:4, :].rearrange("p a d -> p (a d)")[:, 0:2176], [HB[0]], [cb])
    K.op(DVE, lambda e: e.memset(ONES[:], 1.0), [], [cb])
    K.op(DVE, lambda e: e.memset(EPST[:], EPS), [], [cb])
    TS(DVE, GQ[:, :], GQ[:, :], 0.125, None, ALU.mult, None, [cb], [cb])
    TT(DVE, LB[:], LBT[:, 0, :], LBT[:, 1, :], ALU.subtract, [cb], [cb])
    A(OML[:], LB[:], AF.Sigmoid, [cb], [cb], scale=-1.0)
    A(LB[:], LB[:], AF.Sigmoid, [cb], [cb])
    TS(DVE, NOML[:], OML[:], -1.0, None, ALU.mult, None, [cb], [cb])

    stg_i = [0]

    pcB = [Buf(f"pc{i}") for i in range(4)]
    pc_i = [0]
    pc_thunks = []

    def pc_dma(dst, src):
        def th(dst=dst, src=src):
            b = pcB[pc_i[0] % 4]
            pc_i[0] += 1
            K.dma(POOL, dst, src, reads=[], writes=[], sb=b)
        pc_thunks.append(th)

    def pc_mark(name):
        def th():
            b = Buf(name)
            b.w = [(x.dsem, 16 * x.dcount) for x in pcB if x.dsem is not None]
            scrB[name] = b
        pc_thunks.append(th)

    def precast(W, Kdim, N, name, dst_row=None, dst_lhs=None):
        nk = Kdim // 128
        if N == 2 * DFF:
            for j0 in (0, 11):
                for kc in range(nk):
                    rows = W[kc * 128:(kc + 1) * 128, :]
                    for gu in range(2):
                        pc_dma(dst_lhs[j0:j0 + 11, gu, :, kc, :].rearrange("j p c -> p j c"),
                               rows[:, gu * DFF + j0 * 128:gu * DFF + (j0 + 11) * 128].rearrange("p (j c) -> p j c", c=128))
                pc_mark(name + ("a" if j0 == 0 else ""))
            return
        for kc in range(nk):
            rows = W[kc * 128:(kc + 1) * 128, :]
            if dst_row is not None:
                for c0 in range(0, N, 1792):
                    w_ = min(1792, N - c0)
                    pc_dma(dst_row[kc, :, c0:c0 + w_], rows[:, c0:c0 + w_])
            if dst_lhs is not None:
                if N == 2 * DFF:
                    for gu in range(2):
                        for j0 in (0, 11):
                            pc_dma(dst_lhs[j0:j0 + 11, gu, :, kc, :].rearrange("j p c -> p j c"),
                                   rows[:, gu * DFF + j0 * 128:gu * DFF + (j0 + 11) * 128].rearrange("p (j c) -> p j c", c=128))
                else:
                    for j0 in (0, 14):
                        pc_dma(dst_lhs[j0:j0 + 14, :, kc, :].rearrange("j p c -> p j c"),
                               rows[:, j0 * 128:(j0 + 14) * 128].rearrange("p (j c) -> p j c", c=128))
        pc_mark(name)

    precast(w_gu[0], D, 2 * DFF, "gu0", dst_lhs=S_GU[0])
    precast(w_dn[0], DFF, D, "dn0", dst_row=S_DN[0])
    precast(w_in, D, 3584, "in", dst_row=S_INR, dst_lhs=S_INL)
    precast(w_out, D, D, "out", dst_row=S_OUT)
    precast(w_gu[1], D, 2 * DFF, "gu1", dst_lhs=S_GU[1])
    precast(w_dn[1], DFF, D, "dn1", dst_row=S_DN[1])
    for t_ in pc_thunks:
        t_()
    pc_rest = []

    def pc_need(name):
        assert name in scrB

    def stg_final():
        return {}

    def wload(dram_ap, ncols, view=None, scr="in"):
        i = nxt("ring", RING)
        dst = RNG[:, i, 0:ncols]
        if view is not None:
            dst = view(dst)
        LOAD(dst, dram_ap, RNGB[i], reads=[scrB[scr]])
        return RNG[:, i, :], RNGB[i]

    def load_x(hi, src_rows, T):
        nt = max(T // 128, 1)
        if T >= 128:
            LOAD(H[hi][:, 0:nt, :], src_rows.rearrange("(t p) d -> p t d", p=128), HB[hi])
        else:
            LOAD(H[hi][0:T, 0, :], src_rows, HB[hi])

    def norm_T(hi, T, gi):
        norm_apply(hi, T, gi, norm_stats(hi, T))

    def norm_stats(hi, T):
        nt = max(T // 128, 1)
        P = min(T, 128)
        si = nxt("sm", 6)
        for tt in range(nt):
            ti = nxt("tmpa", 3)
            A(TMPA[ti][0:P, :], H[hi][0:P, tt, 0:512], AF.Square, [HB[hi]], [TMPAB[ti], SMB[si]], accum_out=SM[si][0:P, tt:tt + 1])
            A(TMPA[ti][0:P, :], H[hi][0:P, tt, 512:1024], AF.Square, [HB[hi]], [TMPAB[ti], SMB[si]], accum_out=SM[si][0:P, 4 + tt:5 + tt])
        TT(DVE, SM[si][0:P, 8:8 + nt], SM[si][0:P, 0:nt], SM[si][0:P, 4:4 + nt], ALU.add, [SMB[si]], [SMB[si]])
        A(SM[si][0:P, 8:8 + nt], SM[si][0:P, 8:8 + nt], AF.Sqrt, [SMB[si], cb], [SMB[si]], bias=EPST[0:P, 0:1], scale=1.0 / D)
        RECIP(SM[si][0:P, 12:12 + nt], SM[si][0:P, 8:8 + nt], [SMB[si]], [SMB[si]])
        return si

    def norm_apply(hi, T, gi, sis):
        nt = max(T // 128, 1)
        for tt in range(nt):
            P = min(T, 128)
            si = sis
            pt, pb = psumh()
            for hf in range(2):
                A(XN[0:P, hf * 512:(hf + 1) * 512], H[hi][0:P, tt, hf * 512:(hf + 1) * 512], AF.Copy, [HB[hi], SMB[si]], [XNB2[hf]],
                  scale=SM[si][0:P, 12 + tt:13 + tt])
                TR([(pt[:, kc * 128:kc * 128 + P], XN[0:P, kc * 128:(kc + 1) * 128]) for kc in range(4 * hf, 4 * hf + 4)], [XNB2[hf]], [pb])
            TT(DVE, XT[:, :, tt * 128:tt * 128 + P], pt[:, :].rearrange("p (k c) -> p k c", c=128)[:, :, 0:P],
               GT3[:, gi, :].unsqueeze(2).to_broadcast([128, 8, P]), ALU.mult, [pb, cb], [XTB])

    def ffn(hi, T, wi, prefetch=None):
        ffn_gateup(hi, T, wi)
        ffn_down(hi, T, wi, prefetch)

    def ffn_gateup(hi, T, wi, drain=None):
        for j in range(22):
            scr_ = f"gu{wi}a" if j < 11 else f"gu{wi}"
            wg, wgb = wload(S_GU[wi][j, 0], 1024, view=lambda d: d.rearrange("p (k c) -> p k c", c=128), scr=scr_)
            wu, wub = wload(S_GU[wi][j, 1], 1024, view=lambda d: d.rearrange("p (k c) -> p k c", c=128), scr=scr_)
            pg, pgb = psum6()
            pu, pub = psum6()
            MM([(pg[:, 0:T], wg[:, kc * 128:(kc + 1) * 128], XT[:, kc, 0:T], kc == 0, kc == 7) for kc in range(8)], [wgb, XTB], [pgb])
            MM([(pu[:, 0:T], wu[:, kc * 128:(kc + 1) * 128], XT[:, kc, 0:T], kc == 0, kc == 7) for kc in range(8)], [wub, XTB], [pub])
            ti = nxt("tmpa", 3)
            A(TMPA[ti][:, 0:T], pg[:, 0:T], AF.Silu, [pgb], [TMPAB[ti]])
            TT(DVE, ACTT[:, j, 0:T], TMPA[ti][:, 0:T], pu[:, 0:T], ALU.mult, [TMPAB[ti], pub], [ACTTB[j]])
            if drain:
                for _ in range(3):
                    if drain:
                        drain.pop(0)()
            for _ in range(3):
                if pc_rest:
                    pc_rest.pop(0)()
        while drain:
            drain.pop(0)()

    def ffn_down(hi, T, wi, prefetch=None):
        nt = max(T // 128, 1)
        P = min(T, 128)
        if prefetch is not None:
            prefetch()
        for hf in range(2):
            pacc = [psum() for _ in range(nt)]
            for j in range(22):
                wd, wdb = wload(S_DN[wi][j, :, hf * 512:(hf + 1) * 512], 512, scr=f"dn{wi}")
                MM([(pacc[tt][0][0:P, :], ACTT[:, j, tt * 128:tt * 128 + P], wd[:, 0:512], j == 0, j == 21) for tt in range(nt)],
                   [wdb, ACTTB[j]], [pacc[tt][1] for tt in range(nt)])
            for tt in range(nt):
                STT(H[hi][0:P, tt, hf * 512:(hf + 1) * 512], pacc[tt][0][0:P, :], 0.5, H[hi][0:P, tt, hf * 512:(hf + 1) * 512],
                    ALU.mult, ALU.add, [pacc[tt][1], HB[hi]], [HB[hi]])

    def proj_feat(ct, T, t0=0):
        w, wb = wload(S_INL[ct], 1024, view=lambda d: d.rearrange("p (k c) -> p k c", c=128))
        p, pb = psum()
        MM([(p[:, 0:T], w[:, kc * 128:(kc + 1) * 128], XT[:, kc, t0:t0 + T], kc == 0, kc == 7) for kc in range(8)], [wb, XTB], [pb])
        return p, pb

    def proj_tok_weights(c0):
        return [wload(S_INR[kc, :, c0:c0 + 512], 512) for kc in range(8)]

    def proj_tok(ws, cols, M):
        p, pb = psum()
        MM([(p[0:M, :], XT[:, kc, cols[0]:cols[1]], ws[kc][0][:, 0:512], kc == 0, kc == 7) for kc in range(8)],
           [w[1] for w in ws] + [XTB], [pb])
        return p, pb

    SMQ = sb("SMQ", [128, 64])
    smqB = Buf("smq")

    def qk_stats(ps_, P):
        n_ = len(ps_)
        for tt, (p, pb) in enumerate(ps_):
            ti = nxt("tmpa", 3)
            A(TMPA[ti][0:P, :], p[0:P, :], AF.Square, [pb], [TMPAB[ti]])
            K.op(DVE, lambda e, ti=ti, tt=tt: e.tensor_reduce(SMQ[0:P, 8 * tt:8 * tt + 8], TMPA[ti][0:P, :].rearrange("p (h e) -> p h e", e=64), AX.X, ALU.add),
                 [TMPAB[ti]], [smqB])
        A(SMQ[0:P, 32:32 + 8 * n_], SMQ[0:P, 0:8 * n_], AF.Sqrt, [smqB, cb], [smqB], bias=EPST[0:P, 0:1], scale=1.0 / 64)
        RECIP(SMQ[0:P, 0:8 * n_], SMQ[0:P, 32:32 + 8 * n_], [smqB], [smqB])

    def qknorm(p, pb, P, G, out32=None, out32b=None, tt=0):
        ti = nxt("tmpa", 3)
        TT(DVE, TMPA[ti][0:P, :].rearrange("p (h e) -> p h e", e=64), G[0:P, :].unsqueeze(1).to_broadcast([P, 8, 64]),
           SMQ[0:P, 8 * tt:8 * tt + 8].unsqueeze(2).to_broadcast([P, 8, 64]), ALU.mult, [smqB, cb], [TMPAB[ti]])
        bi = nxt("qb16", 2)
        if out32 is not None:
            TT(DVE, out32[0:P, :], p[0:P, :], TMPA[ti][0:P, :], ALU.mult, [pb, TMPAB[ti]], [out32b])
            CP(ACT, QB16[bi][0:P, :], out32[0:P, :], [out32b], [QB16B[bi]])
        else:
            TT(DVE, QB16[bi][0:P, :], p[0:P, :], TMPA[ti][0:P, :], ALU.mult, [pb, TMPAB[ti]], [QB16B[bi]])
        return bi

    def tr_pairs(bi, P, dst, dstb):
        pt, ptb = psumh()
        TR([(pt[:, hp * 128:hp * 128 + P], QB16[bi][0:P, hp * 128:(hp + 1) * 128]) for hp in range(4)], [QB16B[bi]], [ptb])
        CP(ACT, dst, pt[:, 0:512].rearrange("p (k c) -> p k c", c=128)[:, :, 0:P], [ptb], [dstb])

    def attn_kv(T, gtile0, kout=None, vout=None, kvrows=None, halo=False):
        nt = max(T // 128, 1)
        P = min(T, 128)
        wk = proj_tok_weights(512)
        pk = [proj_tok(wk, (tt * 128, tt * 128 + P), P) for tt in range(nt)]
        qk_stats(pk, P)
        for tt in range(nt):
            slot = (gtile0 + tt) % NKS
            qi = nxt("qkv", 3)
            bi = qknorm(pk[tt][0], pk[tt][1], P, GK, out32=QKV[qi], out32b=QKVB[qi], tt=tt)
            if kout is not None:
                STORE(kout[kvrows + tt * 128:kvrows + tt * 128 + P, :], QKV[qi][0:P, :], QKVB[qi])
            tr_pairs(bi, P, KT[:, :, slot * 128:slot * 128 + P], KTB)
        wv = proj_tok_weights(1024)
        pv = [proj_tok(wv, (tt * 128, tt * 128 + P), P) for tt in range(nt)]
        for tt in range(nt):
            slot = (gtile0 + tt) % NKS
            qi = nxt("qkv", 3)
            CP(ACT, QKV[qi][0:P, :], pv[tt][0][0:P, :], [pv[tt][1]], [QKVB[qi]])
            if vout is not None:
                STORE(vout[kvrows + tt * 128:kvrows + tt * 128 + P, :], QKV[qi][0:P, :], QKVB[qi])
            CP(DVE, VP[0:P, slot, :, 0:64], QKV[qi][0:P, :].rearrange("p (h e) -> p h e", e=64), [QKVB[qi]], [VPB])
            if halo:
                CP(DVE, VP[0:P, slot, :, 64:65], FLAG[0:P, 0:1].unsqueeze(1).to_broadcast([P, 8, 1]), [cb], [VPB])
            else:
                K.op(DVE, lambda e, slot=slot: e.memset(VP[0:P, slot, :, 64:65], 1.0), [], [VPB])

    def attn_q(T):
        nt = max(T // 128, 1)
        P = min(T, 128)
        wq = proj_tok_weights(0)
        pq = [proj_tok(wq, (tt * 128, tt * 128 + P), P) for tt in range(nt)]
        qk_stats(pq, P)
        for tt in range(nt):
            bi = qknorm(pq[tt][0], pq[tt][1], P, GQ, tt=tt)
            tr_pairs(bi, P, QT[:, :, tt * 128:tt * 128 + P], QTB)

    def attn_finish(P, aoi, tcol):
        si = nxt("sm", 6)
        ti = nxt("tmpa", 3)
        RECIP(SM[si][0:P, 0:8], AO[0:P, aoi, :, 64], [AOB[aoi]], [SMB[si]])
        TT(DVE, TMPA[ti][0:P, :].rearrange("p (h e) -> p h e", e=64), AO[0:P, aoi, :, 0:64],
           SM[si][0:P, 0:8].unsqueeze(2).to_broadcast([P, 8, 64]), ALU.mult, [AOB[aoi], SMB[si]], [TMPAB[ti]])
        t2 = nxt("tmpa", 3)
        A(TMPA[t2][0:P, :], TMPA[ti][0:P, :], AF.Square, [TMPAB[ti]], [TMPAB[t2], SMB[si]], accum_out=SM[si][0:P, 8:9])
        A(SM[si][0:P, 9:10], SM[si][0:P, 8:9], AF.Sqrt, [SMB[si], cb], [SMB[si]], bias=EPST[0:P, 0:1], scale=1.0 / 512)
        RECIP(SM[si][0:P, 10:11], SM[si][0:P, 9:10], [SMB[si]], [SMB[si]])
        bi = nxt("qb16", 2)
        TS(DVE, QB16[bi][0:P, :], TMPA[ti][0:P, :], SM[si][0:P, 10:11], None, ALU.mult, None, [TMPAB[ti], SMB[si]], [QB16B[bi]])
        pt, ptb = psumh()
        TR([(pt[:, hp * 128:hp * 128 + P], QB16[bi][0:P, hp * 128:(hp + 1) * 128]) for hp in range(4)], [QB16B[bi]], [ptb])
        TT(DVE, CATT[:, 0:4, tcol:tcol + P], pt[:, 0:512].rearrange("p (k c) -> p k c", c=128)[:, :, 0:P],
           GAOT[:, :].unsqueeze(2).to_broadcast([128, 4, P]), ALU.mult, [ptb, cb], CATB[0:4])

    def attn_prompt(gtile0):
        halo_lim = HALO0 * 4 + 16
        for h in range(8):
            hp, base = h // 2, 64 * (h % 2)
            po, pob = psum_acc()
            first = True
            def qk(di, h=h, hp=hp, base=base):
                pS, pSb = psum()
                MM([(pS[:, i * 128:(i + 1) * 128],
                     KT[base:base + 64, hp, ((gtile0 + i - di) % NKS) * 128:((gtile0 + i - di) % NKS) * 128 + 128],
                     QT[base:base + 64, hp, i * 128:(i + 1) * 128], True, True) for i in range(4)], [KTB, QTB], [pSb])
                pi = nxt("pt", 3)
                A(PT[pi][:, :], pS[:, :], AF.Exp, [pSb, cb], [PTB[pi]], bias=AB[:, h, di:di + 1])
                TT(DVE, PT[pi][:, :].rearrange("p (a c) -> p a c", c=128), PT[pi][:, :].rearrange("p (a c) -> p a c", c=128),
                   MULT[:, di:di + 1, :].to_broadcast([128, 4, 128]), ALU.mult, [PTB[pi], cb], [PTB[pi]])
                return pi
            pis = {0: qk(0), 1: qk(1)}
            for di in range(17):
                if di + 2 < 17:
                    pis[di + 2] = qk(di + 2)
                pi = pis.pop(di)
                items = []
                for i in range(4):
                    slot = (gtile0 + i - di) % NKS
                    items.append((po[:, i * 65:(i + 1) * 65], PT[pi][:, i * 128:(i + 1) * 128], VP[:, slot, h, :], first, False))
                    first = False
                MM(items, [PTB[pi], VPB], [pob])
            for i in range(4):
                CP(ACT, AO[:, i, h, :], po[:, i * 65:(i + 1) * 65], [pob], [AOB[i]])
        for i in range(4):
            attn_finish(128, i, i * 128)

    def hgrn(T, C, own, sample=False, t0=0, seq0=0):
        nch = T // C
        mid = C // 2 - 1
        wi_ = proj_tok_weights(2560)
        for c in range(nch):
            p, pb = proj_tok(wi_, (t0 + c * C, t0 + (c + 1) * C), C)
            CP(ACT, VH[0:C, c, :], p[0:C, :], [pb], [VHB[c]])

        def prep(h):
            KT_, KTB_ = KTILH[h % 2], KTILHB[h % 2]
            QT_, QTB_ = KTILH[2 + h % 2], KTILHB[2 + h % 2]
            sc, scB = SCO[h % 2], SCOB[h % 2]
            pf, pfb = proj_feat(16 + h, T, t0)
            hb = [nxt("ht", 8) for _ in range(5)]
            sg, lf, kk, bcs, dd = [HT[i] for i in hb]
            sgB, lfB, kkB, bcsB, ddB = [HTB[i] for i in hb]
            A(sg[:, 0:T], pf[:, 0:T], AF.Sigmoid, [pfb], [sgB])
            A(lf[:, 0:T], sg[:, 0:T], AF.Ln, [sgB, cb], [lfB], bias=LB[:, h:h + 1], scale=OML[:, h:h + 1])
            TS(DVE, kk[:, 0:T], sg[:, 0:T], NOML[:, h:h + 1], OML[:, h:h + 1], ALU.mult, ALU.add, [sgB, cb], [kkB])
            pq, pqb = proj_feat(12 + h, T, t0)
            A(sg[:, 0:T], pq[:, 0:T], AF.Silu, [pqb], [sgB])
            for c in range(nch):
                K.op(DVE, lambda e, c=c, bcs=bcs, lf=lf: e.tensor_tensor_scan(bcs[:, c * C:(c + 1) * C], ONES[:, 0:C], lf[:, c * C:(c + 1) * C], 0.0, ALU.mult, ALU.add),
                     [lfB, cb], [bcsB])
            b3 = bcs[:, 0:T].rearrange("p (n c) -> p n c", c=C)
            d3 = dd[:, 0:T].rearrange("p (n c) -> p n c", c=C)
            TT(DVE, d3, b3, b3[:, :, mid:mid + 1].to_broadcast([128, nch, C]), ALU.subtract, [bcsB], [ddB])
            A(sc[:, 0:nch], b3[:, :, mid], AF.Exp, [bcsB], [scB])
            A(sc[:, 8:8 + nch], b3[:, :, C - 1], AF.Exp, [bcsB], [scB])
            A(sc[:, 16:16 + nch], d3[:, :, C - 1], AF.Exp, [ddB], [scB])
            A(bcs[:, 0:T], dd[:, 0:T], AF.Exp, [ddB], [bcsB])
            A(lf[:, 0:T], dd[:, 0:T], AF.Exp, [ddB], [lfB], scale=-1.0)
            TT(DVE, KT_[:, 0:T], kk[:, 0:T], lf[:, 0:T], ALU.mult, [kkB, lfB], [KTB_])
            TT(DVE, QT_[:, 0:T], sg[:, 0:T], bcs[:, 0:T], ALU.mult, [sgB, bcsB], [QTB_])
            return (KT_, KTB_, QT_, QTB_, sc, scB)

        def chunks(h, P):
            KT_, KTB_, QT_, QTB_, sc, scB = P
            pO, pOb = psum_acc()
            st1, st2 = {}, {}

            def S1(c):
                if sample:
                    sti = nxt("sst", 3)
                    S, SB_ = SST[sti], SSTB[sti]
                    LOAD(S[:], st_in[seq0 + c, h], SB_)
                else:
                    S, SB_ = STATE[h], STATEB[h]
                pt, ptb = psumh()
                TR([(pt[0:C, 0:128], KT_[:, c * C:(c + 1) * C])], [KTB_], [ptb])
                ki = nxt("ktok", 3)
                CP(ACT, KTOK[ki][0:C, :], pt[0:C, 0:128], [ptb], [KTOKB[ki]])
                st1[c] = (S, SB_, ki)

            def S2(c):
                S, SB_, ki = st1[c]
                pS, pSb = psum()
                MM([(pS[:, 0:128], KTOK[ki][0:C, :], VH[0:C, c, h * 128:(h + 1) * 128], True, True)], [KTOKB[ki], VHB[c]], [pSb])
                pA, pAb = psum()
                MM([(pA[0:C, 0:C], KT_[:, c * C:(c + 1) * C], QT_[:, c * C:(c + 1) * C], True, True)], [KTB_, QTB_], [pAb])
                ai = nxt("atm", 3)
                TT(DVE, ATM[ai][0:C, 0:C], pA[0:C, 0:C], TRI[0:C, 0:C], ALU.mult, [pAb, cb], [ATMB[ai]])
                st2[c] = (pS, pSb, ai)

            def S3(c):
                S, SB_, ki = st1[c]
                pS, pSb, ai = st2[c]
                bi = nxt("sbf", 3)
                TS(DVE, SBF[bi][:], S[:], sc[:, c:c + 1], None, ALU.mult, None, [SB_, scB], [SBFB[bi]])
                MM([(pO[:, c * C:(c + 1) * C], VH[0:C, c, h * 128:(h + 1) * 128], ATM[ai][0:C, 0:C], True, False),
                    (pO[:, c * C:(c + 1) * C], SBF[bi][:], QT_[:, c * C:(c + 1) * C], False, True)],
                   [VHB[c], ATMB[ai], SBFB[bi], QTB_], [pOb])
                ei = nxt("spe", 3)
                TS(DVE, SPE[ei][:], pS[:, 0:128], sc[:, 16 + c:17 + c], None, ALU.mult, None, [pSb, scB], [SPEB[ei]])
                STT(S[:], S[:], sc[:, 8 + c:9 + c], SPE[ei][:], ALU.mult, ALU.add, [SB_, scB, SPEB[ei]], [SB_])
                if sample:
                    STORE(st_s[seq0 + c, h], S[:], SB_)
            for step in range(nch + 2):
                if step < nch:
                    S1(step)
                if 0 <= step - 1 < nch:
                    S2(step - 1)
                if 0 <= step - 2 < nch:
                    S3(step - 2)
            return pO, pOb

        def post(h, pO, pOb):
            ta, tb, tc = TMPA[0], TMPA[1], TMPA[2]
            taB, tbB, tcB = TMPAB[0], TMPAB[1], TMPAB[2]
            CP(ACT, ta[:, 0:T], pO[:, 0:T], [pOb], [taB])
            TT(DVE, tb[:, 0:T], ta[:, 0:T], ta[:, 0:T], ALU.mult, [taB], [tbB])
            pn, pnb = psum()
            MM([(pn[:, 0:T], ONES[:, :], tb[:, 0:T], True, True)], [cb, tbB], [pnb])
            A(tb[:, 0:T], pn[:, 0:T], AF.Sqrt, [pnb, cb], [tbB], bias=EPST[:, 0:1], scale=1.0 / 128)
            RECIP(tb[:, 0:T], tb[:, 0:T], [tbB], [tbB])
            pg, pgb = proj_feat(24 + h, T, t0)
            A(tc[:, 0:T], pg[:, 0:T], AF.Silu, [pgb], [tcB])
            STT(ta[:, 0:T], ta[:, 0:T], GHO[:, h:h + 1], tb[:, 0:T], ALU.mult, ALU.mult, [taB, cb, tbB], [taB])
            TT(DVE, CATT[:, 4 + h, t0:t0 + T], ta[:, 0:T], tc[:, 0:T], ALU.mult, [taB, tcB], [CATB[4 + h]])

        P = prep(0)
        for h in range(4):
            Pn = prep(h + 1) if h + 1 < 4 else None
            pO, pOb = chunks(h, P)
            post(h, pO, pOb)
            P = Pn

    SCO = [sb(f"SCO{i}", [128, 24]) for i in range(2)]
    SCOB = [Buf(f"sco{i}") for i in range(2)]
    KTILX = [sb(f"KTILX{i}", [128, GT], BF16) for i in range(2)]
    KTILH = [KTIL, QTIL, KTILX[0], KTILX[1]]
    KTILHB = [KTILB, QTILB, Buf("ktilx0"), Buf("ktilx1")]
    SCH = [sb(f"SCH{i}", [128, 16]) for i in range(4)]
    SCHB = [Buf(f"sch{i}") for i in range(4)]

    def hg_P1():
        wi_ = proj_tok_weights(2560)
        for c in range(8):
            p, pb = proj_tok(wi_, (c * 64, (c + 1) * 64), 64)
            CP(ACT, VH[0:64, c, :], p[0:64, :], [pb], [VHB[c]])
        for h in range(4):
            pf, pfb = proj_feat(16 + h, GT)
            A(HT[h][:, :], pf[:, :], AF.Sigmoid, [pfb], [HTB[h]])

    def hg_E(g):
        C, nch, mid = 64, 8, 31
        rec = []
        K.rec = rec
        for h in range(4):
            sg, sgB = HT[h], HTB[h]
            lf, lfB = HT[4], HTB[4]
            kk, kkB = HT[5], HTB[5]
            bcs, bcsB = HT[6], HTB[6]
            A(lf[:, :], sg[:, :], AF.Ln, [sgB, cb], [lfB], bias=LB[:, h:h + 1], scale=OML[:, h:h + 1])
            TS(DVE, kk[:, :], sg[:, :], NOML[:, h:h + 1], OML[:, h:h + 1], ALU.mult, ALU.add, [sgB, cb], [kkB])
            for c in range(nch):
                K.op(DVE, lambda e, c=c, bcs=bcs, lf=lf: e.tensor_tensor_scan(bcs[:, c * C:(c + 1) * C], ONES[:, 0:C], lf[:, c * C:(c + 1) * C], 0.0, ALU.mult, ALU.add),
                     [lfB, cb], [bcsB])
            b3 = bcs[:, :].rearrange("p (n c) -> p n c", c=C)
            d3 = lf[:, :].rearrange("p (n c) -> p n c", c=C)
            A(SCH[h][:, 0:8], b3[:, :, C - 1], AF.Exp, [bcsB], [SCHB[h]])
            K.op(DVE, lambda e, b3=b3, h=h: e.tensor_reduce(LDT[:, h:h + 1], b3[:, :, C - 1], AX.X, ALU.add), [bcsB], [ldtB])
            STT(LDACC[:, h:h + 1], LDT[:, h:h + 1], GFL[:, g:g + 1], LDACC[:, h:h + 1], ALU.mult, ALU.add, [ldtB, cb, ldB], [ldB])
            TT(DVE, d3, b3, b3[:, :, mid:mid + 1].to_broadcast([128, nch, C]), ALU.subtract, [bcsB], [lfB])
            A(SCH[h][:, 8:16], d3[:, :, C - 1], AF.Exp, [lfB], [SCHB[h]])
            A(bcs[:, :], lf[:, :], AF.Exp, [lfB], [bcsB], scale=-1.0)
            TT(DVE, KTILH[h][:, :], kk[:, :], bcs[:, :], ALU.mult, [kkB, bcsB], [KTILHB[h]])
        K.rec = None
        return rec

    P2BUF = PT + QB16
    P2BUFB = PTB + QB16B

    def hg_P2_start():
        return {i: p2_trb(*p2_order[i]) for i in range(2)}

    def p2_trb(h, half):
        pt, ptb = psumh()
        TR([(pt[0:64, cc * 128:(cc + 1) * 128], KTILH[h][:, (4 * half + cc) * 64:(4 * half + cc + 1) * 64]) for cc in range(4)], [KTILHB[h]], [ptb])
        bi = nxt("p2b", 5)
        CP(ACT, P2BUF[bi][0:64, :], pt[0:64, 0:512], [ptb], [P2BUFB[bi]])
        return bi
    p2_order = [(h, half) for h in range(4) for half in range(2)]

    def hg_P2(bis=None):
        def trb(h, half):
            pt, ptb = psumh()
            TR([(pt[0:64, cc * 128:(cc + 1) * 128], KTILH[h][:, (4 * half + cc) * 64:(4 * half + cc + 1) * 64]) for cc in range(4)], [KTILHB[h]], [ptb])
            bi = nxt("p2b", 5)
            CP(DVE, P2BUF[bi][0:64, :], pt[0:64, 0:512], [ptb], [P2BUFB[bi]])
            return bi
        order = [(h, half) for h in range(4) for half in range(2)]
        LA = 2
        if bis is None:
            bis = {i: trb(*order[i]) for i in range(LA)}
        for idx, (h, half) in enumerate(order):
            if idx + LA < len(order):
                bis[idx + LA] = trb(*order[idx + LA])
            bi = bis.pop(idx)
            if True:
                pS, pSb = psum_acc()
                MM([(pS[:, cc * 128:(cc + 1) * 128], P2BUF[bi][0:64, cc * 128:(cc + 1) * 128], VH[0:64, 4 * half + cc, h * 128:(h + 1) * 128], True, True)
                    for cc in range(4)], [P2BUFB[bi]] + VHB[4 * half:4 * half + 4], [pSb])
                for cc in range(4):
                    c = 4 * half + cc
                    ei = nxt("spe", 3)
                    A(SPE[ei][:], pS[:, cc * 128:(cc + 1) * 128], AF.Copy, [pSb, SCHB[h]], [SPEB[ei]], scale=SCH[h][:, 8 + c:9 + c])
                    STT(STATE[h][:], STATE[h][:], SCH[h][:, c:c + 1], SPE[ei][:], ALU.mult, ALU.add, [STATEB[h], SCHB[h], SPEB[ei]], [STATEB[h]])

    def outproj(hi, T):
        nt = max(T // 128, 1)
        P = min(T, 128)
        for hf in range(2):
            ws = [wload(S_OUT[ch, :, hf * 512:(hf + 1) * 512], 512, scr="out") for ch in range(8)]
            for tt in range(nt):
                p, pb = psum()
                MM([(p[0:P, :], CATT[:, ch, tt * 128:tt * 128 + P], ws[ch][0][:, 0:512], ch == 0, ch == 7) for ch in range(8)],
                   [w[1] for w in ws] + CATB, [pb])
                TT(DVE, H[hi][0:P, tt, hf * 512:(hf + 1) * 512], p[0:P, :], H[hi][0:P, tt, hf * 512:(hf + 1) * 512], ALU.add, [pb, HB[hi]], [HB[hi]])

    def attn_sample():
        K.op(DVE, lambda e: e.memset(VP[:, 0:7, :, 64:65], 1.0), [], [VPB])
        K.op(DVE, lambda e: e.memset(VP[:, 8:15, :, 64:65], 1.0), [], [VPB])
        KTs = [Buf("kts0"), Buf("kts1")]
        VPs = [Buf("vps0"), Buf("vps1")]
        for x_ in KTs:
            x_.w = list(KTB.w)
            x_.r = dict(KTB.r)
        for x_ in VPs:
            x_.w = list(VPB.w)
            x_.r = dict(VPB.r)
        for b in range(16):
            s0 = 8 * (b % 2)
            for tile in range(7):
                for (src, isk) in ((ck, True), (cv, False)):
                    ci = nxt("ht", 8)
                    if tile < 4:
                        LOAD(CST[ci][:], src[b, 1536 + 128 * tile:1536 + 128 * tile + 128, :], CSTB[ci])
                    else:
                        u = tile - 4
                        v = src[b].rearrange("(m s) c -> m s c", s=16)
                        K.dma_multi(SP, [(CST[ci][32 * t_:32 * t_ + 32, :], v[32 * u:32 * u + 32, t_, :]) for t_ in range(4)],
                                    reads=[], writes=[CSTB[ci]], sb=CSTB[ci])
                    if isk:
                        bi = nxt("qb16", 2)
                        CP(DVE, CB16[bi][:], CST[ci][:], [CSTB[ci]], [CB16B[bi]])
                        pt, ptb = psumh()
                        TR([(pt[:, hp * 128:(hp + 1) * 128], CB16[bi][:, hp * 128:(hp + 1) * 128]) for hp in range(4)], [CB16B[bi]], [ptb])
                        CP(ACT, KT[:, :, (s0 + tile) * 128:(s0 + tile + 1) * 128], pt[:, 0:512].rearrange("p (k c) -> p k c", c=128), [ptb], [KTs[b % 2]])
                    else:
                        CP(ACT, VP[:, s0 + tile, :, 0:64], CST[ci][:].rearrange("p (h e) -> p h e", e=64), [CSTB[ci]], [VPs[b % 2]])
            pS, pSb = psum()
            items = []
            for h in range(8):
                hp, base = h // 2, 64 * (h % 2)
                for tile in range(7):
                    items.append((pS[:, (h * 7 + tile) * 4:(h * 7 + tile) * 4 + 4], KT[base:base + 64, hp, (s0 + tile) * 128:(s0 + tile + 1) * 128],
                                  QT[base:base + 64, hp, 4 * b:4 * b + 4], True, True))
                items.append((pS[0:64, 224 + h * 4:224 + h * 4 + 4], KT[base:base + 64, hp, 7 * 128:7 * 128 + 64],
                              QT[base:base + 64, hp, 4 * b:4 * b + 4], True, True))
            MM(items, [KTB, KTs[b % 2], QTB], [pSb])
            pi = nxt("pt", 3)
            ti = nxt("tmpa", 3)
            A(TMPA[ti][:, 0:224], pS[:, 0:224], AF.Exp, [pSb], [TMPAB[ti]])
            A(TMPA[ti][0:64, 224:256], pS[0:64, 224:256], AF.Exp, [pSb], [TMPAB[ti]])
            TT(DVE, PT[pi][:, 0:224], TMPA[ti][:, 0:224], WS[:, :], ALU.mult, [TMPAB[ti], cb], [PTB[pi]])
            TT(DVE, PT[pi][0:64, 224:256], TMPA[ti][0:64, 224:256], WN[:, b, :], ALU.mult, [TMPAB[ti], cb], [PTB[pi]])
            pos = [psum_acc(), psum_acc()]
            for half in range(2):
                items = []
                first = True
                for hh in range(4):
                    h = half * 4 + hh
                    for tile in range(7):
                        items.append((pos[half][0][0:4, hh * 65:(hh + 1) * 65], PT[pi][:, (h * 7 + tile) * 4:(h * 7 + tile) * 4 + 4],
                                      VP[:, s0 + tile, h, :], first, False))
                        first = False
                    items.append((pos[half][0][0:4, hh * 65:(hh + 1) * 65], PT[pi][0:64, 224 + h * 4:224 + h * 4 + 4], VP[0:64, 7, h, :], False, False))
                MM(items, [PTB[pi], VPB, VPs[b % 2]], [pos[half][1]])
            qi = 1 + b % 3
            for half in range(2):
                CP(ACT, AO[0:4, qi, 4 * half:4 * half + 4, :], pos[half][0][0:4, 0:260].rearrange("p (h e) -> p h e", e=65), [pos[half][1]], [AOB[qi]])
            STORE(S_AO[4 * b:4 * b + 4, :], AO[0:4, qi, :, :].rearrange("p h e -> p (h e)"), AOB[qi], writes=[aoB])
        aoB.w = [(b_.dsem, 16 * b_.dcount) for b_ in AOB[1:4] if b_.dsem is not None]
        LOAD(AO[0:64, 0, :, :].rearrange("p h e -> p (h e)"), S_AO, AOB[0], reads=[aoB])
        attn_finish(64, 0, 0)

    def group(hi, T, kind, gidx, src_rows, out_rows, nextload, skip_norm1=False):
        if not skip_norm1:
            norm_T(hi, T, 0)
        ffn(hi, T, 0, prefetch=nextload)
        norm_T(hi, T, 1)
        if kind == 3:
            attn_kv(T, 7, kout=k_s, vout=v_s, kvrows=0)
            attn_q(T)
            attn_sample()
            hgrn(32, 4, True, sample=True, t0=0, seq0=0)
            hgrn(32, 4, True, sample=True, t0=32, seq0=8)
        else:
            if kind >= 1:
                attn_kv(T, gidx * 4, kout=k_p if kind == 2 else None, vout=v_p if kind == 2 else None,
                        kvrows=(gidx - OWN0) * GT if kind == 2 else None, halo=(kind == 1))
            if kind == 2:
                attn_q(T)
                attn_prompt(gidx * 4)
            hgrn(T, 64, kind == 2)
        if kind >= 2:
            outproj(hi, T)
            norm_T(hi, T, 2)
            ffn(hi, T, 1)
            if kind == 3:
                STORE(out_rows, H[hi][0:T, 0, :], HB[hi])
            else:
                STORE(out_rows.rearrange("(t p) d -> p t d", p=128), H[hi][:, :, :], HB[hi])

    for h in range(4):
        K.op(DVE, lambda e, h=h: e.memset(STATE[h][:], 0.0), [], [STATEB[h]])
    load_x(1, xp[0:GT, :], GT)

    parc = {}

    def get_par(e):
        if "p" not in parc:
            parc["p"] = e.partition_id() % 2
        return parc["p"]

    def sh_row(e, mine, seg):
        par = get_par(e)
        who = par if mine else ((par + 1) % 2)
        return SH_S2[bass.ds(who * 256 + seg * 128, 128), :]

    def sh_l(e, mine):
        par = get_par(e)
        who = par if mine else ((par + 1) % 2)
        return SH_L2[bass.ds(who * 128, 128), :]

    def seg_boundary():
        K.dma(POOL, (lambda e: sh_row(e, True, 0)), ST4[:].rearrange("p h c -> p (h c)"), reads=STATEB, writes=[], sb=STATEB[0])
        CP(DVE, LDM[:, 0:4], LDACC[:, :], [ldB], [ldmB])
        K.op(DVE, lambda e: e.memset(LDACC[:], 0.0), [], [ldB])
        for h in range(4):
            K.op(DVE, lambda e, h=h: e.memset(STATE[h][:], 0.0), [], [STATEB[h]])

    def exchange():
        CP(DVE, LDM[:, 4:8], LDACC[:, :], [ldB], [ldmB])
        K.dma(POOL, (lambda e: sh_row(e, True, 1)), ST4[:].rearrange("p h c -> p (h c)"), reads=STATEB, writes=[], sb=STATEB[0])
        K.dma(POOL, (lambda e: sh_l(e, True)), LDM[:, :], reads=[ldmB], writes=[], sb=ldmB)
        pubB = Buf("pub")
        pubB.w = [(b.dsem, 16 * b.dcount) for b in [STATEB[0], ldmB]]
        flgB = Buf("flg")
        K.dma(POOL, (lambda e: SH_F[bass.ds(get_par(e), 1), :]), tok, reads=[pubB], writes=[flgB], sb=flgB)

        K.dma(POOL, HT[4][:], (lambda e: sh_row(e, True, 0)), reads=[], writes=[HTB[4]], sb=HTB[4])

        def cond(e):
            kw = dict(allow_slow_non_contiguous=True)
            with e.register("rx") as rx, e.register("rf0") as rf0, e.register("rf1") as rf1, e.register("rt") as rt, e.register("rd") as rd:
                e.load(rx, needx[0:1, 0:1])
                with e.If_ne(rx, 0):
                    e.load(rt, tok[0:1, 0:1])
                    e.reg_mov(rd, 1)
                    with e.While(rd):
                        e.load(rf0, SH_F[0:1, 0:1])
                        e.load(rf1, SH_F[1:2, 0:1])
                        e.reg_sub(rf0, rf0, rt)
                        e.reg_sub(rf1, rf1, rt)
                        e.reg_alu(rd, rf0, rf1, ALU.bitwise_or)
                    e.dma_start(out=LDP[:, :], in_=sh_l(e, False), **kw).then_inc(ldpB.dsem, 16)
                    e.dma_start(out=HT[5][:], in_=sh_row(e, False, 0), **kw).then_inc(HTB[5].dsem, 16)
                    e.dma_start(out=HT[6][:], in_=sh_row(e, False, 1), **kw).then_inc(HTB[6].dsem, 16)
                with e.Else():
                    e.dma_start(out=LDP[:, :], in_=zer[:, 0:8], **kw).then_inc(ldpB.dsem, 16)
                    e.dma_start(out=HT[5][:], in_=zer[:, :], **kw).then_inc(HTB[5].dsem, 16)
                    e.dma_start(out=HT[6][:], in_=zer[:, :], **kw).then_inc(HTB[6].dsem, 16)
        K.raw(POOL, cond, reads=[flgB], writes=[ldpB, HTB[5], HTB[6]], dma_bufs=[ldpB, HTB[5], HTB[6]])
        A(LDM[:, :], LDM[:, :], AF.Exp, [ldmB], [ldmB])
        A(LDP[:, :], LDP[:, :], AF.Exp, [ldpB], [ldpB])
        for h in range(4):
            S1m, S1p, S2p = HT[4][:, h * 128:(h + 1) * 128], HT[5][:, h * 128:(h + 1) * 128], HT[6][:, h * 128:(h + 1) * 128]
            D1m, D2m = LDM[:, h:h + 1], LDM[:, 4 + h:5 + h]
            D1p, D2p = LDP[:, h:h + 1], LDP[:, 4 + h:5 + h]
            STT(SPE[0][:], S1p, D1m, S1m, ALU.mult, ALU.add, [HTB[4], HTB[5], ldmB], [SPEB[0]])
            STT(SPE[1][:], S1m, D1p, S1p, ALU.mult, ALU.add, [HTB[4], HTB[5], ldpB], [SPEB[1]])
            STT(SPE[1][:], SPE[1][:], D2p, S2p, ALU.mult, ALU.add, [SPEB[1], HTB[6], ldpB], [SPEB[1]])
            TS(DVE, SPE[0][:], SPE[0][:], ROLE[:, 0:1], None, ALU.mult, None, [SPEB[0], cb], [SPEB[0]])
            STT(SPE[1][:], SPE[1][:], ROLE[:, 1:2], SPE[0][:], ALU.mult, ALU.add, [SPEB[1], SPEB[0], cb], [SPEB[1]])
            STT(STATE[h][:], SPE[1][:], D2m, STATE[h][:], ALU.mult, ALU.add, [SPEB[1], ldmB, STATEB[h]], [STATEB[h]])

    pendE, pendP2 = [], False
    norm_T(1, GT, 0)
    for g in range(NG):
        hi = (g + 1) % 2
        kind = 2 if g >= OWN0 else (1 if g >= HALO0 else 0)
        nl = (lambda g=g, hi=hi: load_x(1 - hi, xp[(g + 1) * GT:(g + 2) * GT, :], GT)) if g + 1 < NG else (lambda: load_x(1, xs, 64))
        orow = y_p[(g - OWN0) * GT:(g - OWN0 + 1) * GT, :] if kind == 2 else None
        if kind == 2:
            while pendE:
                pendE.pop(0)()
            if pendP2:
                hg_P2()
                pendP2 = False
            if g == OWN0:
                pc_need("dn1")
                exchange()
            group(hi, GT, kind, g, None, orow, nl, skip_norm1=(g == OWN0))
            continue
        ffn_gateup(hi, GT, 0, drain=pendE)
        pc_need("dn0")
        ffn_down(hi, GT, 0, prefetch=nl)
        p2s = hg_P2_start() if pendP2 else None
        sis1 = norm_stats(hi, GT)
        if pendP2:
            hg_P2(p2s)
        if g == HALO0:
            seg_boundary()
        pc_need("in")
        norm_apply(hi, GT, 1, sis1)
        if kind == 1:
            if g == HALO0:
                KTB.r.update(stg_final())
                VPB.r.update(stg_final())
            attn_kv(GT, g * 4, halo=True)
        sis = norm_stats(1 - hi, GT)
        hg_P1()
        pendE = hg_E(g)
        pendP2 = True
        norm_apply(1 - hi, GT, 0, sis)
    for h in range(4):
        STORE(st_p[h], STATE[h][:], STATEB[h])
    group(1, 64, 3, 0, xs, y_s, None)

    final_waits = [(b.dsem, 16 * b.dcount) for b in K.dbufs]
    engs = {"tensor": PE, "scalar": ACT, "vector": DVE, "gpsimd": POOL, "sync": SP}
    with nc.Block() as block:
        def emit(e, eng, final=False):
            for (wl, fn, sem, inc) in eng.prog:
                for (s_, v_) in wl:
                    e.wait_ge(s_, v_)
                if sem is None:
                    fn(e)
                else:
                    fn(e).then_inc(sem, inc)
            if final:
                for (s_, v_) in final_waits:
                    e.wait_ge(s_, v_)

        @block.tensor
        def _(e):
            emit(e, PE)

        @block.scalar
        def _(e):
            emit(e, ACT)

        @block.vector
        def _(e):
            emit(e, DVE)

        @block.gpsimd
        def _(e):
            emit(e, POOL, final=True)

        @block.sync
        def _(e):
            emit(e, SP)
    es.close()
    return nc


def _consts(core):
    slopes = np.array([2.0 ** (-8.0 * (h + 1) / 8) for h in range(8)], np.float64)

    def mult(dist):
        dist = np.asarray(dist)
        m = ((dist >= 0) & (dist <= 128)).astype(np.float64)
        m += ((dist >= 0) & (dist % 4 == 0) & (dist <= 512))
        m += ((dist >= 0) & (dist % 16 == 0) & (dist <= 2048))
        return m
    ki = np.arange(128)[:, None]
    qi = np.arange(128)[None, :]
    c_mult = np.zeros((128, 17, 128), np.float32)
    for di in range(17):
        c_mult[:, di, :] = mult(128 * di + qi - ki)
    c_ab = np.zeros((128, 8, 17), np.float32)
    for h in range(8):
        for di in range(17):
            c_ab[:, h, di] = -slopes[h] * (128 * di + 64 - np.arange(128))
    rows = np.zeros((7, 128), np.int64)
    for t in range(4):
        rows[t] = 1536 + 128 * t + np.arange(128)
    for u in range(3):
        p = np.arange(128)
        rows[4 + u] = 16 * (32 * u + p % 32) + p // 32
    c_ws = np.zeros((128, 8, 7, 4), np.float64)
    for h in range(8):
        for tile in range(7):
            for t in range(4):
                dist = 2048 + t - rows[tile]
                c_ws[:, h, tile, t] = mult(dist) * np.exp(-slopes[h] * dist)
    c_wn = np.zeros((64, 16, 8, 4), np.float64)
    for j in range(64):
        b_, t_ = j // 4, j % 4
        for t in range(4):
            if t_ <= t:
                dist = t - t_
                for h in range(8):
                    c_wn[j, b_, h, t] = mult(dist) * np.exp(-slopes[h] * dist)
    c_tri = (np.arange(64)[:, None] <= np.arange(64)[None, :]).astype(np.float32)
    return dict(
        c_mult=c_mult.reshape(128, -1), c_ab=c_ab.reshape(128, -1),
        c_ws=c_ws.reshape(128, -1).astype(np.float32), c_wn=c_wn.reshape(64, -1).astype(np.float32),
        c_tri=c_tri, c_id=np.eye(128, dtype=np.float32),
        c_flag=np.full((128, 1), 0.0 if core == 0 else 1.0, np.float32),
    )


_NC = None


def kernel(x_prompt, x_sample, cache_k, cache_v, state_hgrn, norm_ffn1, ffn1_w_gate_up, ffn1_w_down, norm_mix, w_in,
           q_norm, k_norm, gamma_lb, attn_out_norm, hgrn_out_norm, w_out, norm_ffn2, ffn2_w_gate_up, ffn2_w_down):
    global _NC
    if _NC is None:
        _NC = build_program()
    nc = _NC
    f = lambda a: np.ascontiguousarray(np.asarray(a, dtype=np.float32))
    xpf = f(x_prompt)[0]
    in_maps = []
    token = np.full((1, 16), (int.from_bytes(os.urandom(4), "little") & 0x7FFFFFFF) | 1, np.int32)
    for c in range(NCORE):
        k, r = c // 2, c % 2
        n = max(2 * k - 1, 0)
        nB = (n + 1) // 2
        seg1 = list(range(0, nB)) if r == 1 else list(range(nB, n))
        seg2 = (2 * k) if r == 1 else ((2 * k - 1) if k >= 1 else None)
        xp_c = np.zeros((NSL * SLICE, D), np.float32)
        gfl = np.zeros((128, 16), np.float32)
        for i, sl in enumerate(seg1):
            pos = 3 - len(seg1) + i
            xp_c[pos * SLICE:(pos + 1) * SLICE] = xpf[sl * SLICE:(sl + 1) * SLICE]
            gfl[:, pos * 4:(pos + 1) * 4] = 1.0
        if seg2 is not None:
            xp_c[3 * SLICE:4 * SLICE] = xpf[seg2 * SLICE:(seg2 + 1) * SLICE]
            gfl[:, 12:16] = 1.0
        xp_c[4 * SLICE:5 * SLICE] = xpf[c * SLICE:(c + 1) * SLICE]
        role = np.zeros((128, 2), np.float32)
        role[:, 0] = 1.0 if r == 0 else 0.0
        role[:, 1] = 1.0 - role[:, 0]
        m = dict(
            xp=xp_c, xs=f(x_sample)[16 * c:16 * c + 16].reshape(64, D),
            ck=f(cache_k)[0, 16 * c:16 * c + 16].reshape(16, 2048, 512),
            cv=f(cache_v)[0, 16 * c:16 * c + 16].reshape(16, 2048, 512),
            st=f(state_hgrn)[0, 16 * c:16 * c + 16],
            w_gu1=f(ffn1_w_gate_up)[0], w_d1=f(ffn1_w_down)[0], w_gu2=f(ffn2_w_gate_up)[0], w_d2=f(ffn2_w_down)[0],
            w_in=f(w_in)[0], w_out=f(w_out)[0], g1=f(norm_ffn1), gm=f(norm_mix), g2=f(norm_ffn2),
            gq=f(q_norm), gk=f(k_norm), glb=f(gamma_lb), gao=f(attn_out_norm), gho=f(hgrn_out_norm),
        )
        m.update(_consts(c))
        m.update(c_gfl=gfl, c_role=role, tok=token, needx=np.full((1, 16), 1 if c >= 2 else 0, np.int32),
                 zer=np.zeros((128, 512), np.float32))
        in_maps.append(m)
    res = run_bass_kernel_spmd(nc, in_maps, core_ids=list(range(NCORE)))
    R = res.results
    y_prompt = np.concatenate([R[c]["y_p"] for c in range(NCORE)], 0)[None]
    y_sample = np.concatenate([R[c]["y_s"].reshape(16, 4, D) for c in range(NCORE)], 0)
    nkp = R[7]["k_p"].reshape(1, 1, 2048, 8, 64)
    nvp = R[7]["v_p"].reshape(1, 1, 2048, 8, 64)
    nsp = R[7]["st_p"].reshape(1, 1, 4, 128, 128)
    nks = np.concatenate([R[c]["k_s"].reshape(16, 4, 8, 64) for c in range(NCORE)], 0)[None]
    nvs = np.concatenate([R[c]["v_s"].reshape(16, 4, 8, 64) for c in range(NCORE)], 0)[None]
    nss = np.concatenate([R[c]["st_s"] for c in range(NCORE)], 0)[None]
    return (y_prompt.astype(np.float32), y_sample.astype(np.float32), nkp.astype(np.float32), nvp.astype(np.float32),
            nsp.astype(np.float32), nks.astype(np.float32), nvs.astype(np.float32), nss.astype(np.float32))
```
